# Optimizing a Trainium2 kernel written in Bass

```python
import math
import jax
import jax.numpy as jnp
from jax import lax
import numpy as np

D_MODEL = 1024
BATCH = 32
SEQ = 2048
DEPTH = 4

CTX_LEN = 256
GRID_W = 64
EPS = 1e-6
NEG_INF = -1e30
F32 = jnp.float32

MIX_WIDTH = D_MODEL
RET_HEADS = 4
RET_V = MIX_WIDTH // 4
RET_DV = RET_V // RET_HEADS
RET_DK = RET_DV // 2
RET_QK = RET_HEADS * RET_DK
RET_CHUNK = 128
RET_ROT_BASE = 10000.0
HY_CH = MIX_WIDTH // 4
HY_EMB = 33
HY_BANDS = (HY_EMB - 1) // 2
HY_ORDER = 64
HY_INNER = 2
HY_FAST_DECAY = 0.3
HY_SLOW_DECAY = 1.5
HY_TARGET = 1e-2
ATT_HEADS = 8
ATT_KV_HEADS = 2
ATT_GROUP = ATT_HEADS // ATT_KV_HEADS
ATT_Q = MIX_WIDTH - RET_V - HY_CH
ATT_HD = ATT_Q // ATT_HEADS
ATT_KV = ATT_KV_HEADS * ATT_HD
WINDOW = 128
ATT_BLOCK = 128
ATT_HALO = -(-WINDOW // ATT_BLOCK)
ROPE_BASE = 10000.0
D_FF = 4 * D_MODEL

SPLIT_SIZES = (RET_QK, RET_QK, RET_V, RET_V, 3 * HY_CH, ATT_Q, ATT_KV, ATT_KV)
D_IN = 2 * RET_QK + 2 * RET_V + 3 * HY_CH + ATT_Q + 2 * ATT_KV

kernel_name = 'hybrid_retention_hyena_swa_prefix_dit'


def rms_norm(x, g=None):
    xf = x.astype(F32)
    y = xf * lax.rsqrt(jnp.mean(xf * xf, axis=-1, keepdims=True) + EPS)
    if g is not None:
        y = y * g.astype(F32)
    return y.astype(x.dtype)


def modulate(h, shift, scale):
    return h * (1.0 + scale) + shift


def rotate_half(x):
    x1, x2 = jnp.split(x, 2, axis=-1)
    return jnp.concatenate([-x2, x1], axis=-1)


def rotary_table(pos_f, inv_freq):
    ang = pos_f[:, None] * inv_freq[None, :]
    ang = jnp.concatenate([ang, ang], axis=-1)
    return jnp.cos(ang), jnp.sin(ang)


def apply_rotary(x, cos, sin):
    cos = cos[None, :, None, :].astype(x.dtype)
    sin = sin[None, :, None, :].astype(x.dtype)
    return x * cos + rotate_half(x) * sin


def apply_axial(x, tabs):
    cos_r, sin_r, cos_c, sin_c = tabs
    half = x.shape[-1] // 2
    return jnp.concatenate([apply_rotary(x[..., :half], cos_r, sin_r),
                            apply_rotary(x[..., half:], cos_c, sin_c)], axis=-1)


def split_proj(p):
    out, o = [], 0
    for s in SPLIT_SIZES:
        out.append(p[..., o:o + s])
        o += s
    return out


def sq_relu_mlp(h, w1, w2):
    return jnp.square(jax.nn.relu(h @ w1)) @ w2


def retention_chunkwise(q, k, v, log_g, s0):
    B, L, H, DK = q.shape
    DV = v.shape[-1]
    C = RET_CHUNK
    N = L // C
    qc = q.reshape(B, N, C, H, DK)
    kc = k.reshape(B, N, C, H, DK)
    vc = v.reshape(B, N, C, H, DV)
    idx = jnp.arange(C, dtype=F32)
    diff = idx[:, None] - idx[None, :]
    dmask = jnp.where(diff >= 0, jnp.exp(log_g[:, None, None] * jnp.maximum(diff, 0.0)[None]), 0.0)
    scores = jnp.einsum('bnihd,bnjhd->bnhij', qc, kc) * dmask
    o_inner = jnp.einsum('bnhij,bnjhe->bnihe', scores, vc)
    k_dec = jnp.exp(log_g[None, :] * (C - 1 - idx)[:, None])
    kv = jnp.einsum('bnjhd,jh,bnjhe->nbhde', kc, k_dec, vc)
    chunk_dec = jnp.exp(log_g * C)[None, :, None, None]

    def step(s, kv_n):
        return chunk_dec * s + kv_n, s

    _, s_start = lax.scan(step, s0, kv)
    q_dec = jnp.exp(log_g[None, :] * (idx + 1.0)[:, None])
    o_cross = jnp.einsum('bnihd,ih,nbhde->bnihe', qc, q_dec, s_start)
    return (o_inner + o_cross).reshape(B, L, H, DV)


def retention_bidir(q, k, v, log_f, log_b, s_f, s_b):
    fwd = retention_chunkwise(q, k, v, log_f, s_f)
    bwd = retention_chunkwise(jnp.flip(q, 1), jnp.flip(k, 1), jnp.flip(v, 1), log_b, s_b)
    return fwd + jnp.flip(bwd, 1)


def retention_ctx_states(k, v, log_f, log_b):
    m = jnp.arange(k.shape[1], dtype=F32)
    w_f = jnp.exp(log_f[None, :] * (k.shape[1] - 1.0 - m)[:, None])
    w_b = jnp.exp(log_b[None, :] * m[:, None])
    s_f = jnp.einsum('bmhd,mh,bmhe->bhde', k, w_f, v)
    s_b = jnp.einsum('bmhd,mh,bmhe->bhde', k, w_b, v)
    return s_f, s_b


def short_conv(u, w, b):
    up = jnp.pad(u, ((0, 0), (1, 1), (0, 0)))
    return up[:, :-2] * w[0] + up[:, 1:-1] * w[1] + up[:, 2:] * w[2] + b


def hyena_filters(L, w1, b1, w2, b2, w3, freq):
    t = jnp.linspace(0.0, 1.0, L, dtype=F32)[:, None]
    w = 2.0 * math.pi * jnp.arange(L, dtype=F32) / L
    bands = jnp.linspace(1e-4, HY_BANDS - 1, HY_BANDS, dtype=F32)
    ang = w[:, None] * bands[None, :]
    z = jnp.concatenate([t, jnp.cos(ang), -jnp.sin(ang)], axis=-1)
    fr = freq.astype(F32)
    h = jnp.sin(fr * (z @ w1.astype(F32) + b1.astype(F32)))
    for j in range(HY_INNER):
        h = jnp.sin(fr * (h @ w2[j].astype(F32) + b2[j].astype(F32)))
    h = h @ w3.astype(F32)
    max_decay = math.log(HY_TARGET) / HY_FAST_DECAY
    min_decay = math.log(HY_TARGET) / HY_SLOW_DECAY
    deltas = jnp.linspace(min_decay, max_decay, HY_CH, dtype=F32)
    decay = jnp.exp(-t * jnp.abs(deltas)[None, :])
    h = h.reshape(L, 2, HY_CH) * decay[:, None, :]
    return h[:, 0], h[:, 1]


def long_conv(z, h_f, h_b):
    L = z.shape[1]
    n = 2 * L
    filt = jnp.concatenate([h_f, jnp.zeros((1, h_f.shape[1]), F32), h_b[1:][::-1]], axis=0)
    zf = jnp.fft.rfft(z.astype(F32), n=n, axis=1)
    hf = jnp.fft.rfft(filt, n=n, axis=0)
    y = jnp.fft.irfft(zf * hf[None], n=n, axis=1)[:, :L]
    return y.astype(z.dtype)


def hyena_mix(u, short_w, short_b, filt, hy_bias):
    u = short_conv(u, short_w, short_b)
    v, x1, x0 = jnp.split(u, 3, axis=-1)
    h_f, h_b = hyena_filters(u.shape[1], *filt)
    z = v * x1
    y = long_conv(z, h_f, h_b) + z * hy_bias
    return y * x0


def sink_attend(q, k, v, bias, sink):
    s = jnp.einsum('bqhgd,bkhd->bhgqk', q, k).astype(F32) * (ATT_HD ** -0.5) + bias
    sk = sink[None, :, :, None, None]
    m = jnp.maximum(jnp.max(s, axis=-1, keepdims=True), sk)
    p = jnp.exp(s - m)
    den = jnp.sum(p, axis=-1, keepdims=True) + jnp.exp(sk - m)
    return jnp.einsum('bhgqk,bkhd->bqhgd', (p / den).astype(v.dtype), v)


def window_attention(q, k, v, kc, vc, sink):
    B, L = q.shape[0], q.shape[1]
    nb = L // ATT_BLOCK
    nband = 2 * ATT_HALO + 1
    pad = ATT_HALO * ATT_BLOCK
    qb = jnp.moveaxis(q.reshape(B, nb, ATT_BLOCK, ATT_KV_HEADS, ATT_GROUP, ATT_HD), 1, 0)

    def band(t):
        tp = jnp.pad(t, ((0, 0), (pad, pad), (0, 0), (0, 0)))
        blocks = [tp[:, s * ATT_BLOCK: s * ATT_BLOCK + L].reshape(B, nb, ATT_BLOCK, ATT_KV_HEADS, ATT_HD)
                  for s in range(nband)]
        return jnp.moveaxis(jnp.concatenate(blocks, axis=2), 1, 0)

    kb, vb = band(k), band(v)
    ii = jnp.arange(ATT_BLOCK)[:, None]
    jj = jnp.arange(nband * ATT_BLOCK)[None, :]
    in_window = jnp.abs(jj - pad - ii) <= WINDOW
    key_pos = jnp.arange(nb)[:, None] * ATT_BLOCK - pad + jnp.arange(nband * ATT_BLOCK)[None, :]
    in_range = (key_pos >= 0) & (key_pos < L)
    band_bias = jnp.where(in_window[None] & in_range[:, None, :], 0.0, NEG_INF).astype(F32)
    ctx_bias = jnp.zeros((ATT_BLOCK, kc.shape[1]), F32)

    def one_block(args):
        qn, kn, vn, bn = args
        return sink_attend(qn, jnp.concatenate([kn, kc], axis=1), jnp.concatenate([vn, vc], axis=1),
                           jnp.concatenate([bn, ctx_bias], axis=-1), sink)

    ob = lax.map(one_block, (qb, kb, vb, band_bias))
    return jnp.moveaxis(ob, 0, 1).reshape(B, L, ATT_Q)


def trunk_layer(x, xc, sc, scc, axial_tabs, ret_tabs, lp, last):
    (w_ada, b_ada, g_pre_mix, g_post_mix, g_pre_mlp, g_post_mlp, w_in, dec_f, dec_b,
     short_w, short_b, f_w1, f_b1, f_w2, f_b2, f_w3, f_freq, hy_bias, attn_sink,
     g_ret, g_hy, g_att, w_out, w_ff1, w_ff2) = lp
    B, L, _ = x.shape
    Lc = xc.shape[1]
    sh_a, sc_a, gt_a, sh_m, sc_m, gt_m = jnp.split((sc @ w_ada + b_ada)[:, None, :], 6, axis=-1)
    csh_a, csc_a, cgt_a, csh_m, csc_m, cgt_m = jnp.split((scc @ w_ada + b_ada)[None, None, :], 6, axis=-1)
    log_f = jnp.log1p(-jnp.exp(dec_f.astype(F32)))
    log_b = jnp.log1p(-jnp.exp(dec_b.astype(F32)))
    sink = attn_sink.astype(F32).reshape(ATT_KV_HEADS, ATT_GROUP)
    filt = (f_w1, f_b1, f_w2, f_b2, f_w3, f_freq)
    k_scale = RET_DK ** -0.5

    px = modulate(rms_norm(x, g_pre_mix), sh_a, sc_a) @ w_in
    pc = modulate(rms_norm(xc, g_pre_mix), csh_a, csc_a) @ w_in
    rq, rk, rv, rg, hy, aq, ak, av = split_proj(px)
    crq, crk, crv, crg, chy, caq, cak, cav = split_proj(pc)

    crk = crk.reshape(B, Lc, RET_HEADS, RET_DK).astype(F32) * k_scale
    crv = crv.reshape(B, Lc, RET_HEADS, RET_DV).astype(F32)
    s_f, s_b = retention_ctx_states(crk, crv, log_f, log_b)
    rq = apply_rotary(rq.reshape(B, L, RET_HEADS, RET_DK), *ret_tabs).astype(F32)
    rk = apply_rotary(rk.reshape(B, L, RET_HEADS, RET_DK), *ret_tabs).astype(F32) * k_scale
    rv = rv.reshape(B, L, RET_HEADS, RET_DV).astype(F32)
    ret = retention_bidir(rq, rk, rv, log_f, log_b, s_f, s_b)
    ret = rms_norm(ret).reshape(B, L, RET_V).astype(x.dtype) * jax.nn.silu(rg)

    hyo = hyena_mix(hy, short_w, short_b, filt, hy_bias)

    aq = apply_axial(aq.reshape(B, L, ATT_HEADS, ATT_HD), axial_tabs)
    ak = apply_axial(ak.reshape(B, L, ATT_KV_HEADS, ATT_HD), axial_tabs)
    av = av.reshape(B, L, ATT_KV_HEADS, ATT_HD)
    cak = cak.reshape(B, Lc, ATT_KV_HEADS, ATT_HD)
    cav = cav.reshape(B, Lc, ATT_KV_HEADS, ATT_HD)
    att = window_attention(aq, ak, av, cak, cav, sink)

    mix = jnp.concatenate([rms_norm(ret, g_ret), rms_norm(hyo, g_hy), rms_norm(att, g_att)], axis=-1) @ w_out
    x = x + gt_a * rms_norm(mix, g_post_mix)

    if not last:
        zeros = jnp.zeros((B, RET_HEADS, RET_DK, RET_DV), F32)
        crq = crq.reshape(B, Lc, RET_HEADS, RET_DK).astype(F32)
        cret = retention_bidir(crq, crk, crv, log_f, log_b, zeros, zeros)
        cret = rms_norm(cret).reshape(B, Lc, RET_V).astype(xc.dtype) * jax.nn.silu(crg)
        chyo = hyena_mix(chy, short_w, short_b, filt, hy_bias)
        catt = sink_attend(caq.reshape(B, Lc, ATT_KV_HEADS, ATT_GROUP, ATT_HD), cak, cav, 0.0, sink)
        catt = catt.reshape(B, Lc, ATT_Q)
        cmix = jnp.concatenate([rms_norm(cret, g_ret), rms_norm(chyo, g_hy), rms_norm(catt, g_att)], axis=-1) @ w_out
        xc = xc + cgt_a * rms_norm(cmix, g_post_mix)

    hm = sq_relu_mlp(modulate(rms_norm(x, g_pre_mlp), sh_m, sc_m), w_ff1, w_ff2)
    x = x + gt_m * rms_norm(hm, g_post_mlp)
    if not last:
        hcm = sq_relu_mlp(modulate(rms_norm(xc, g_pre_mlp), csh_m, csc_m), w_ff1, w_ff2)
        xc = xc + cgt_m * rms_norm(hcm, g_post_mlp)
    return x, xc


def setup_inputs(seed: int = 0) -> dict:
    key = jax.random.key(seed)
    ks = iter(jax.random.split(key, 32))

    def nrm(shape, scale):
        return scale * jax.random.normal(next(ks), shape, F32)

    ret_base = (-(5.0 + jnp.arange(RET_HEADS, dtype=F32)) * math.log(2.0))[None, :]
    return {
        'x': nrm((BATCH, SEQ, D_MODEL), 1.0),
        'c': nrm((BATCH, D_MODEL), 1.0),
        'ctx': nrm((BATCH, CTX_LEN, D_MODEL), 1.0),
        'c_ctx': nrm((D_MODEL,), 1.0),
        'w_ada': nrm((DEPTH, D_MODEL, 6 * D_MODEL), 0.5 * D_MODEL ** -0.5),
        'b_ada': nrm((DEPTH, 6 * D_MODEL), 0.02),
        'g_pre_mix': 1.0 + nrm((DEPTH, D_MODEL), 0.05),
        'g_post_mix': 1.0 + nrm((DEPTH, D_MODEL), 0.05),
        'g_pre_mlp': 1.0 + nrm((DEPTH, D_MODEL), 0.05),
        'g_post_mlp': 1.0 + nrm((DEPTH, D_MODEL), 0.05),
        'w_in': nrm((DEPTH, D_MODEL, D_IN), D_MODEL ** -0.5),
        'ret_decay_fwd': ret_base + nrm((DEPTH, RET_HEADS), 0.1),
        'ret_decay_bwd': ret_base + nrm((DEPTH, RET_HEADS), 0.1),
        'hy_short_w': nrm((DEPTH, 3, 3 * HY_CH), 3.0 ** -0.5),
        'hy_short_b': nrm((DEPTH, 3 * HY_CH), 0.02),
        'hy_f_w1': nrm((DEPTH, HY_EMB, HY_ORDER), HY_EMB ** -0.5),
        'hy_f_b1': nrm((DEPTH, HY_ORDER), 0.1),
        'hy_f_w2': nrm((DEPTH, HY_INNER, HY_ORDER, HY_ORDER), HY_ORDER ** -0.5),
        'hy_f_b2': nrm((DEPTH, HY_INNER, HY_ORDER), 0.1),
        'hy_f_w3': nrm((DEPTH, HY_ORDER, 2 * HY_CH), HY_ORDER ** -0.5),
        'hy_f_freq': 1.0 + nrm((DEPTH, HY_ORDER), 0.05),
        'hy_bias': nrm((DEPTH, HY_CH), 0.5),
        'attn_sink': nrm((DEPTH, ATT_HEADS), 0.5),
        'g_ret': 1.0 + nrm((DEPTH, RET_V), 0.05),
        'g_hy': 1.0 + nrm((DEPTH, HY_CH), 0.05),
        'g_att': 1.0 + nrm((DEPTH, ATT_Q), 0.05),
        'w_out': nrm((DEPTH, MIX_WIDTH, D_MODEL), MIX_WIDTH ** -0.5),
        'w_ff1': nrm((DEPTH, D_MODEL, D_FF), D_MODEL ** -0.5),
        'w_ff2': nrm((DEPTH, D_FF, D_MODEL), D_FF ** -0.5),
    }


def reference(x, c, ctx, c_ctx, w_ada, b_ada, g_pre_mix, g_post_mix, g_pre_mlp, g_post_mlp,
              w_in, ret_decay_fwd, ret_decay_bwd, hy_short_w, hy_short_b, hy_f_w1, hy_f_b1,
              hy_f_w2, hy_f_b2, hy_f_w3, hy_f_freq, hy_bias, attn_sink, g_ret, g_hy, g_att,
              w_out, w_ff1, w_ff2):
    n_tok = x.shape[1]
    ROWS = n_tok // GRID_W
    pos = jnp.arange(ROWS * GRID_W)
    row = (pos // GRID_W).astype(F32)
    col = (pos % GRID_W).astype(F32)
    half = ATT_HD // 2
    inv_ax = ROPE_BASE ** (-jnp.arange(0, half, 2, dtype=F32) / half)
    cos_r, sin_r = rotary_table(row, inv_ax)
    cos_c, sin_c = rotary_table(col, inv_ax)
    axial_tabs = (cos_r, sin_r, cos_c, sin_c)
    inv_ret = 1.0 / (RET_ROT_BASE ** jnp.linspace(0.0, 1.0, RET_DK // 2, dtype=F32))
    ret_tabs = rotary_table(pos.astype(F32), inv_ret)

    sc = jax.nn.silu(c)
    scc = jax.nn.silu(c_ctx)
    xc = ctx
    for i in range(DEPTH):
        lp = (w_ada[i], b_ada[i], g_pre_mix[i], g_post_mix[i], g_pre_mlp[i], g_post_mlp[i], w_in[i],
              ret_decay_fwd[i], ret_decay_bwd[i], hy_short_w[i], hy_short_b[i], hy_f_w1[i], hy_f_b1[i],
              hy_f_w2[i], hy_f_b2[i], hy_f_w3[i], hy_f_freq[i], hy_bias[i], attn_sink[i],
              g_ret[i], g_hy[i], g_att[i], w_out[i], w_ff1[i], w_ff2[i])
        x, xc = trunk_layer(x, xc, sc, scc, axial_tabs, ret_tabs, lp, i == DEPTH - 1)
    return x
```

```python
import math
import numpy as np
import ml_dtypes
import concourse.bass as bass
import concourse.mybir as mybir
from concourse.bass_utils import run_bass_kernel_spmd

F32 = mybir.dt.float32
BF16 = mybir.dt.bfloat16
AF = mybir.ActivationFunctionType
ALU = mybir.AluOpType
AX = mybir.AxisListType

D = 1024
L = 2048
LC = 256
TOK = L + LC
NT = TOK // 128
DEPTH = 4
D_IN = 2304
D_FF = 4096
EPS = 1e-6
HY_CH = 256
ATT_M = 20.0
GROUPS = [(0, 512), (512, 512), (1024, 512), (1536, 512), (2048, 256)]
FFN_PASSES = [[(0, 512), (512, 256)], [(768, 512), (1280, 256)], [(1536, 512), (2048, 256)]]


def tiles_of(t0, n):
    return list(range(t0 // 128, (t0 + n + 127) // 128))


class Prog:
    ENGS = ("pe", "act", "dve", "pool", "sp")

    def __init__(self, nc, eng_sems, dma_sems):
        self.nc = nc
        self.eng_sem = eng_sems
        self.dma_sems = dma_sems
        self.dma_val = {q: [0] * len(v) for q, v in dma_sems.items()}
        self.dma_rr = {q: 0 for q in dma_sems}
        self.count = {e: 0 for e in self.ENGS}
        self.ops = {e: [] for e in self.ENGS}
        self.last_write = {}
        self.readers = {}
        self.waited = {}
        self.semid = {}
        self.n_ops = 0

    def _sid(self, sem):
        return id(sem)

    def _deps(self, reads, writes, eng=None):
        deps = []
        for k in reads:
            t = self.last_write.get(k)
            if t is not None:
                deps.append(t)
            if k[0] == "ps":
                deps.extend(tk for tk in self.readers.get(k, ()) if tk[2] != eng)
        for k in writes:
            t = self.last_write.get(k)
            if t is not None:
                deps.append(t)
            deps.extend(self.readers.get(k, ()))
        return deps

    def _record(self, reads, writes, token):
        for k in writes:
            self.last_write[k] = token
            self.readers[k] = []
        for k in reads:
            self.readers.setdefault(k, []).append(token)

    def _filter(self, eng, deps):
        out = []
        for (sem, val, src) in deps:
            if eng == "pe" and src == "pe":
                continue
            key = (eng, self._sid(sem))
            if self.waited.get(key, 0) >= val:
                continue
            self.waited[key] = val
            out.append((sem, val))
        return out

    def op(self, eng, meth, args, kw, reads=(), writes=()):
        fn = (meth, args, kw)
        deps = self._deps(reads, writes, eng)
        waits = self._filter(eng, deps)
        self.count[eng] += 1
        sem = self.eng_sem[eng]
        token = (sem, self.count[eng], eng)
        self.ops[eng].append((waits, fn, sem, 1))
        self._record(reads, writes, token)
        self.n_ops += 1
        return token

    def dma(self, q, out_ap, in_ap, reads=(), writes=()):
        k = self.dma_rr[q]
        self.dma_rr[q] = (k + 1) % len(self.dma_sems[q])
        sem = self.dma_sems[q][k]
        prev = self.dma_val[q][k]
        deps = self._deps(reads, writes)
        if prev > 0:
            deps.append((sem, prev, "dma"))
        waits = self._filter(q, deps)
        val = prev + 16
        self.dma_val[q][k] = val
        token = (sem, val, "dma")
        self.ops[q].append((waits, ("dma_start", (), {"out": out_ap, "in_": in_ap}), sem, 16))
        self._record(reads, writes, token)
        self.n_ops += 1
        return token

    def barrier(self, skip_q=()):
        toks = []
        for e in self.ENGS:
            if self.count[e] > 0:
                toks.append((self.eng_sem[e], self.count[e], e))
        for q, sems in self.dma_sems.items():
            if q in skip_q:
                continue
            for s, v in zip(sems, self.dma_val[q]):
                if v > 0:
                    toks.append((s, v, "dma"))
        for e in self.ENGS:
            waits = []
            for (sem, val, src) in toks:
                if src == e:
                    continue
                key = (e, self._sid(sem))
                if self.waited.get(key, 0) >= val:
                    continue
                self.waited[key] = val
                waits.append((sem, val))
            if waits:
                self.ops[e].append((waits, None, None, 0))
        self.last_write = {}
        self.readers = {}

    def emit(self, block):
        nc = self.nc
        engmap = {"pe": block.tensor, "act": block.scalar, "dve": block.vector, "pool": block.gpsimd, "sp": block.sync}
        for ename in self.ENGS:
            ops = self.ops[ename]

            def body(e, ops=ops):
                for (waits, fn, sem, inc) in ops:
                    for (s, v) in waits:
                        e.wait_ge(s, v)
                    if fn is not None:
                        meth, args, kw = fn
                        getattr(e, meth)(*args, **kw).then_inc(sem, inc)
            engmap[ename](body)


def _bf(a):
    return np.ascontiguousarray(a.astype(np.float32)).astype(ml_dtypes.bfloat16)


def make_consts():
    c = {}
    f32 = np.float32
    c["ident_f"] = np.eye(128, dtype=f32)
    c["ident_b"] = _bf(np.eye(128))
    c["ones_b"] = _bf(np.ones((128, 128)))
    P = np.zeros((128, 128), f32)
    for fp in range(128):
        if fp % 32 < 16:
            P[fp + 16, fp] = -1.0
        else:
            P[fp - 16, fp] = 1.0
    c["prot_f"] = P
    hm = np.zeros((128, 4), f32)
    for p in range(128):
        hm[p, p // 32] = 1.0
    c["hm"] = hm
    bd = np.zeros((128, 256), f32)
    for p in range(128):
        h = p // 32
        bd[p, h * 64:(h + 1) * 64] = 1.0
    c["bd"] = bd
    j = np.arange(128)[:, None].astype(f32)
    i = np.arange(128)[None, :].astype(f32)
    c["posd"] = np.maximum(i - j, 0.0).astype(f32)
    c["negd"] = np.maximum(j - i, 0.0).astype(f32)
    c["ge"] = (i >= j).astype(f32)
    c["le"] = (i <= j).astype(f32)
    c["ip1"] = np.broadcast_to(i + 1.0, (128, 128)).astype(f32).copy()
    c["cmi"] = np.broadcast_to(128.0 - i, (128, 128)).astype(f32).copy()
    cols = np.zeros((128, 4), f32)
    cols[:, 0] = 127.0 - np.arange(128)
    cols[:, 1] = np.arange(128)
    cols[:, 2] = 1.0
    cols[0, 2] = 0.0
    cols[:, 3] = 1.0
    c["cols"] = cols
    c["mprev"] = _bf((j >= i).astype(f32))
    c["mnext"] = _bf((j <= i).astype(f32))
    pos = np.arange(L, dtype=np.float64)
    inv_ret = 1.0 / (10000.0 ** np.linspace(0.0, 1.0, 16))
    ang = pos[:, None] * inv_ret[None, :].astype(np.float32).astype(np.float64)
    ang = np.concatenate([ang, ang], -1)
    ang = (pos.astype(np.float32)[:, None] * inv_ret.astype(np.float32)[None, :]).astype(np.float32)
    ang = np.concatenate([ang, ang], -1)
    cr, sr = np.cos(ang).astype(f32), np.sin(ang).astype(f32)
    ks = 32.0 ** -0.5
    tab = np.zeros((4, 128, TOK), f32)
    for h in range(4):
        tab[0, h * 32:(h + 1) * 32, :L] = cr.T
        tab[1, h * 32:(h + 1) * 32, :L] = sr.T
    tab[0, :, L:] = 1.0
    tab[2] = tab[0] * ks
    tab[3] = tab[1] * ks
    c["rtab"] = tab
    half = 32
    inv_ax = (10000.0 ** (-np.arange(0, half, 2, dtype=np.float32) / half)).astype(f32)
    row = (np.arange(L) // 64).astype(f32)
    col = (np.arange(L) % 64).astype(f32)
    ar = np.concatenate([row[:, None] * inv_ax[None, :]] * 2, -1)
    ac = np.concatenate([col[:, None] * inv_ax[None, :]] * 2, -1)
    atab = np.zeros((2, 128, TOK), f32)
    for hh in range(2):
        atab[0, hh * 64:hh * 64 + 32, :L] = np.cos(ar).T
        atab[0, hh * 64 + 32:hh * 64 + 64, :L] = np.cos(ac).T
        atab[1, hh * 64:hh * 64 + 32, :L] = np.sin(ar).T
        atab[1, hh * 64 + 32:hh * 64 + 64, :L] = np.sin(ac).T
    atab[0, :, L:] = 1.0
    c["atab"] = atab
    for name, Lh in (("l", L), ("c", LC)):
        t = np.linspace(0.0, 1.0, Lh, dtype=f32)[:, None]
        w = (2.0 * math.pi * np.arange(Lh, dtype=f32) / Lh).astype(f32)
        bands = np.linspace(1e-4, 15, 16, dtype=f32)
        angz = w[:, None] * bands[None, :]
        z = np.concatenate([t, np.cos(angz), -np.sin(angz)], -1).astype(f32)
        c["zembT_" + name] = np.ascontiguousarray(z.T)
        max_decay = math.log(1e-2) / 0.3
        min_decay = math.log(1e-2) / 1.5
        deltas = np.linspace(min_decay, max_decay, HY_CH, dtype=f32)
        decay = np.exp(-t * np.abs(deltas)[None, :]).astype(f32)
        ntc = Lh // 128
        c["decay_" + name] = np.ascontiguousarray(decay.reshape(ntc, 128, HY_CH).transpose(1, 0, 2))
        k = np.arange(Lh, dtype=np.float64)[:, None]
        tt = np.arange(Lh, dtype=np.float64)[None, :]
        ang2 = (math.pi / Lh) * (k + 0.5) * tt
        C = np.cos(ang2)
        S = np.sin(ang2)
        nk = Lh // 128
        fw = np.zeros((nk, 2, 128, ntc, 128), np.float32)
        for kc in range(nk):
            Ct = C[kc * 128:(kc + 1) * 128, :].T.reshape(ntc, 128, 128).transpose(1, 0, 2)
            St = S[kc * 128:(kc + 1) * 128, :].T.reshape(ntc, 128, 128).transpose(1, 0, 2)
            fw[kc, 0] = Ct
            fw[kc, 1] = St
        c["dftf_" + name] = _bf(fw)
        tb = min(512, Lh)
        ntb = Lh // tb
        iv = np.zeros((ntb, 128, nk, 2, tb), np.float32)
        for b in range(ntb):
            Cb = (C[:, b * tb:(b + 1) * tb] / Lh).reshape(nk, 128, tb).transpose(1, 0, 2)
            Sb = (-S[:, b * tb:(b + 1) * tb] / Lh).reshape(nk, 128, tb).transpose(1, 0, 2)
            iv[b, :, :, 0, :] = Cb
            iv[b, :, :, 1, :] = Sb
        c["dfti_" + name] = _bf(iv)
    return c


_CONSTS = None


def get_consts():
    global _CONSTS
    if _CONSTS is None:
        _CONSTS = make_consts()
    return _CONSTS


W_NAMES = [
    ("w_ada", [DEPTH, D, 6 * D]), ("b_ada", [DEPTH, 6 * D]), ("g_pre_mix", [DEPTH, D]), ("g_post_mix", [DEPTH, D]),
    ("g_pre_mlp", [DEPTH, D]), ("g_post_mlp", [DEPTH, D]), ("w_in", [DEPTH, D, D_IN]),
    ("ret_decay_fwd", [DEPTH, 4]), ("ret_decay_bwd", [DEPTH, 4]), ("hy_short_w", [DEPTH, 3, 768]),
    ("hy_short_b", [DEPTH, 768]), ("hy_f_w1", [DEPTH, 33, 64]), ("hy_f_b1", [DEPTH, 64]),
    ("hy_f_w2", [DEPTH, 2, 64, 64]), ("hy_f_b2", [DEPTH, 2, 64]), ("hy_f_w3", [DEPTH, 64, 512]),
    ("hy_f_freq", [DEPTH, 64]), ("hy_bias", [DEPTH, 256]), ("attn_sink", [DEPTH, 8]), ("g_ret", [DEPTH, 256]),
    ("g_hy", [DEPTH, 256]), ("g_att", [DEPTH, 512]), ("w_out", [DEPTH, D, D]), ("w_ff1", [DEPTH, D, D_FF]),
    ("w_ff2", [DEPTH, D_FF, D]),
]

ARENA_BYTES = 56320
PA1, PS1, PG1, PA2, PS2, PG2 = range(6)


class Builder:
    def __init__(self, NB=4, NL=4, dbg=None):
        self.NB, self.NL, self.dbg = NB, NL, dbg
        self.NBX = NB + 1
        self.nc = bass.Bass("TRN2", target_bir_lowering=False)
        self.C = get_consts()

    def av(self, off, shape, dt):
        esz = 4 if dt == F32 else 2
        nel = int(np.prod(shape[1:]))
        assert off % 4 == 0 and off + nel * esz <= ARENA_BYTES, (off, shape)
        a = self.ARENA[0:shape[0], off // 2: off // 2 + nel * esz // 2]
        if dt == F32:
            a = a.bitcast(F32)
        if len(shape) == 3:
            a = a.rearrange("p (a b) -> p a b", a=shape[1])
        elif len(shape) == 4:
            a = a.rearrange("p (a b c) -> p a b c", a=shape[1], b=shape[2])
        return a

    def bv(self, off_el, shape):
        nel = int(np.prod(shape[1:]))
        a = self.BIG[:, off_el: off_el + nel]
        if len(shape) == 3:
            a = a.rearrange("p (a b) -> p a b", a=shape[1])
        return a

    def bigv(self, off, shape, dt, base_el=10 * TOK):
        esz = 4 if dt == F32 else 2
        nel = int(np.prod(shape[1:]))
        e0 = base_el + off // 2
        assert off % 4 == 0 and e0 + nel * esz // 2 <= 36864, (off, shape)
        a = self.BIG[0:shape[0], e0: e0 + nel * esz // 2]
        if dt == F32:
            a = a.bitcast(F32)
        if len(shape) == 3:
            a = a.rearrange("p (a b) -> p a b", a=shape[1])
        return a

    def psb(self):
        i = self.ps_rr
        self.ps_rr = (i + 1) % 8
        return i

    def O(self, eng, meth, *args, r=(), w=(), **kw):
        return self.p.op(eng, meth, args, kw, list(r), list(w))

    def DMA(self, q, out_ap, in_ap, r=(), w=()):
        return self.p.dma(q, out_ap, in_ap, list(r), list(w))

    def build(self):
        from contextlib import ExitStack
        nc = self.nc
        NB, NBX = self.NB, self.NBX
        self.d = {}
        self.d["x"] = nc.dram_tensor("x", [NB, L, D], F32, kind="ExternalInput").ap()
        self.d["ctx"] = nc.dram_tensor("ctx", [NB, LC, D], F32, kind="ExternalInput").ap()
        self.d["c"] = nc.dram_tensor("c", [NB, D], F32, kind="ExternalInput").ap()
        self.d["c_ctx"] = nc.dram_tensor("c_ctx", [1, D], F32, kind="ExternalInput").ap()
        for name, shp in W_NAMES:
            self.d[name] = nc.dram_tensor(name, shp, F32, kind="ExternalInput").ap()
        self.k = {}
        for name, arr in self.C.items():
            dt = F32 if arr.dtype == np.float32 else BF16
            self.k[name] = nc.dram_tensor("k_" + name, list(arr.shape), dt, kind="ExternalInput").ap()
        self.out_d = nc.dram_tensor("out", [NB, L, D], F32, kind="ExternalOutput").ap()
        self.hf = {"l": nc.dram_tensor("hf_l", [DEPTH, 128, L // 128, 2, 256], BF16, kind="Internal").ap(),
                   "c": nc.dram_tensor("hf_c", [DEPTH, 128, LC // 128, 2, 256], BF16, kind="Internal").ap()}
        self.wb = {}
        for name, shp in (("w_in", [DEPTH, D, D_IN]), ("w_out", [DEPTH, D, D]), ("w_ff1", [DEPTH, D, D_FF]), ("w_ff2", [DEPTH, D_FF, D])):
            self.wb[name] = nc.dram_tensor("wb_" + name, shp, BF16, kind="Internal").ap()
        if self.dbg:
            self.dbg_x = nc.dram_tensor("dbg_x", [128, 8 * TOK], F32, kind="ExternalOutput").ap()
            self.dbg_big = nc.dram_tensor("dbg_big", [128, 36864], BF16, kind="ExternalOutput").ap()
        with ExitStack() as es:
            def sb(name, shape, dt):
                return es.enter_context(nc.sbuf_tensor(name, shape, dt))
            self.X = sb("X", [128, 8, TOK], F32)
            self.BIG = sb("BIG", [128, 36864], BF16)
            self.ARENA = sb("ARENA", [128, ARENA_BYTES // 2], BF16)
            self.ident_f = sb("ident_f", [128, 128], F32)
            self.ident_b = sb("ident_b", [128, 128], BF16)
            self.ones_b = sb("ones_b", [128, 128], BF16)
            self.prot_f = sb("prot_f", [128, 128], F32)
            self.hm = sb("hm", [128, 4], F32)
            self.bd = sb("bd", [128, 256], F32)
            self.cols = sb("cols", [128, 4], F32)
            self.mprev = sb("mprev", [128, 128], BF16)
            self.mnext = sb("mnext", [128, 128], BF16)
            self.ones_row = sb("ones_row", [1, 128], F32)
            self.COLS = sb("COLS", [128, DEPTH, 66], F32)
            self.PAR = sb("PAR", [128, DEPTH, 6, 8, NBX], F32)
            self.LOGG = sb("LOGG", [128, 32], F32)
            self.SINKE = sb("SINKE", [128, 32], F32)
            self.HC = sb("HC", [64, DEPTH, 8], F32)
            self.PS = [es.enter_context(nc.psum_tensor("ps%d" % i, [128, 512], F32)) for i in range(8)]
            self.ps_rr = 0
            eng_sems = {e: es.enter_context(nc.semaphore("s_" + e)) for e in Prog.ENGS}
            dma_sems = {q: [es.enter_context(nc.semaphore("d_%s%d" % (q, i))) for i in range(8)] for q in ("sp", "pool")}
            self.p = Prog(nc, eng_sems, dma_sems)
            self.HT = self.bv(0, [128, 8, TOK])
            self.MIXT = self.bv(8 * TOK, [128, 8, TOK])
            self.program()
            with nc.Block() as block:
                self.p.emit(block)
        return nc

    def program(self):
        p = self.p
        self.prologue()
        for b in range(self.NB):
            self.load_x(b)
            for l in range(self.NL):
                self.layer(b, l)
            if self.dbg:
                self.dump_dbg()
            self.store_out(b)
        p.barrier()

    def prologue(self):
        p, O, DMA = self.p, self.O, self.DMA
        NB, NBX = self.NB, self.NBX
        for name in ("ident_f", "ident_b", "ones_b", "prot_f", "hm", "bd", "cols", "mprev", "mnext"):
            t = getattr(self, name)
            DMA("sp", t[:], self.k[name][:, :], w=[(name,)])
        DMA("sp", self.ones_row[0:1, :], self.k["ip1"][0:1, :], w=[("ones_row_raw",)])
        O("dve", "tensor_scalar", self.ones_row[0:1, :], self.ones_row[0:1, :], 0.0, 1.0, ALU.mult, ALU.add,
          r=[("ones_row_raw",)], w=[("ones_row",)])
        for l in range(self.NL):
            for name, rows in (("w_in", D), ("w_out", D), ("w_ff1", D), ("w_ff2", D_FF)):
                for r0 in range(0, rows, 256):
                    DMA("pool", self.wb[name][l, r0:r0 + 256, :], self.d[name][l, r0:r0 + 256, :], w=[("wb", name, l, r0)])
        CS = self.av(0, [NBX, 1024], F32)
        CSb = self.av(4096, [NBX, 1024], F32)
        SCT = self.av(10752, [128, 8, NBX], F32)
        DMA("sp", CS[0:NB, :], self.d["c"][:, :], w=[("CS", 0)])
        DMA("sp", CS[NB:NBX, :], self.d["c_ctx"][:, :], w=[("CS", 1)])
        O("act", "activation", CSb[:, :], CS[:, :], AF.Silu, r=[("CS", 0), ("CS", 1)], w=[("CSb",)])
        pb = self.psb()
        psf = self.PS[pb][:, 0:64].rearrange("p (k c) -> p k c", k=8)
        for kc in range(8):
            O("pe", "transpose", psf[:, kc, 0:NBX], CSb[0:NBX, kc * 128:(kc + 1) * 128], self.ident_f[0:NBX, 0:NBX],
              r=[("CSb",), ("ident_f",)], w=[("ps", pb)])
        O("dve", "tensor_copy", SCT[:, :, :], psf[:, :, 0:NBX], r=[("ps", pb)], w=[("SCT",)])
        STG = self.av(8192, [128, 128], F32)
        BST = self.av(8704, [48, 128], F32)
        MODT = self.av(9216, [128, 48, NBX], F32)
        BADA = self.av(9216 + 48 * NBX * 4, [128, 48], F32)
        HST = self.av(10496, [4, 64], F32)
        slab_off = 12288
        rowmap = [("g_pre_mix", 0, 8), ("g_post_mix", 8, 8), ("g_pre_mlp", 16, 8), ("g_post_mlp", 24, 8),
                  ("g_ret", 32, 2), ("g_hy", 34, 2), ("g_att", 36, 4), ("hy_short_b", 58, 6), ("hy_bias", 64, 2)]
        for l in range(self.NL):
            for name, r0, nr in rowmap:
                DMA("sp", STG[r0:r0 + nr, :], self.d[name][l].rearrange("(r c) -> r c", c=128), w=[("STG", name)])
            DMA("sp", STG[40:58, :], self.d["hy_short_w"][l].rearrange("k (r c) -> (k r) c", c=128), w=[("STG", "sw")])
            pb = self.psb()
            O("pe", "transpose", self.PS[pb][:, 0:66], STG[0:66, :], self.ident_f[0:66, 0:66],
              r=[("STG", n) for n, _, _ in rowmap] + [("STG", "sw"), ("ident_f",)], w=[("ps", pb)])
            O("dve", "tensor_copy", self.COLS[:, l, :], self.PS[pb][:, 0:66], r=[("ps", pb)], w=[("COLS", l)])
            DMA("sp", BST[0:48, :], self.d["b_ada"][l].rearrange("(r c) -> r c", c=128), w=[("BST",)])
            pb = self.psb()
            O("pe", "transpose", self.PS[pb][:, 0:48], BST[0:48, :], self.ident_f[0:48, 0:48], r=[("BST",), ("ident_f",)], w=[("ps", pb)])
            O("dve", "tensor_copy", BADA[:, :], self.PS[pb][:, 0:48], r=[("ps", pb)], w=[("BADA",)])
            for s in range(12):
                slab = self.av(slab_off + (s % 2) * 16384, [128, 8, 512], F32)
                DMA("sp", slab[:, :, :], self.d["w_ada"][l, :, s * 512:(s + 1) * 512].rearrange("(kc p) n -> p kc n", p=128),
                    w=[("adaslab", s % 2)])
                for oc in range(4):
                    pb = self.psb()
                    for kc in range(8):
                        O("pe", "matmul", self.PS[pb][:, 0:NBX], slab[:, kc, oc * 128:(oc + 1) * 128], SCT[:, kc, :],
                          start=(kc == 0), stop=(kc == 7), r=[("adaslab", s % 2), ("SCT",)], w=[("ps", pb)])
                    og = s * 4 + oc
                    O("dve", "tensor_scalar", MODT[:, og, :], self.PS[pb][:, 0:NBX], BADA[:, og:og + 1], None, ALU.add,
                      r=[("ps", pb), ("BADA",)], w=[("MODT", og)])
            allmod = [("MODT", i) for i in range(48)]

            def colb(r0):
                return self.COLS[:, l, r0:r0 + 8].unsqueeze(2).to_broadcast([128, 8, NBX])
            PARl = self.PAR[:, l]
            O("dve", "scalar_tensor_tensor", PARl[:, PA1], MODT[:, 8:16, :], 1.0, colb(0), ALU.add, ALU.mult,
              r=allmod + [("COLS", l)], w=[("PAR", l, PA1)])
            O("dve", "tensor_copy", PARl[:, PS1], MODT[:, 0:8, :], r=allmod, w=[("PAR", l, PS1)])
            O("dve", "tensor_tensor", PARl[:, PG1], MODT[:, 16:24, :], colb(8), ALU.mult, r=allmod + [("COLS", l)], w=[("PAR", l, PG1)])
            O("dve", "scalar_tensor_tensor", PARl[:, PA2], MODT[:, 32:40, :], 1.0, colb(16), ALU.add, ALU.mult,
              r=allmod + [("COLS", l)], w=[("PAR", l, PA2)])
            O("dve", "tensor_copy", PARl[:, PS2], MODT[:, 24:32, :], r=allmod, w=[("PAR", l, PS2)])
            O("dve", "tensor_tensor", PARl[:, PG2], MODT[:, 40:48, :], colb(24), ALU.mult, r=allmod + [("COLS", l)], w=[("PAR", l, PG2)])
            DMA("sp", HST[0:1, :], self.d["hy_f_b1"][l:l + 1, :], w=[("HST", 0)])
            DMA("sp", HST[1:3, :], self.d["hy_f_b2"][l], w=[("HST", 1)])
            DMA("sp", HST[3:4, :], self.d["hy_f_freq"][l:l + 1, :], w=[("HST", 2)])
            pb = self.psb()
            O("pe", "transpose", self.PS[pb][0:64, 0:4], HST[0:4, :], self.ident_f[0:4, 0:4],
              r=[("HST", 0), ("HST", 1), ("HST", 2), ("ident_f",)], w=[("ps", pb)])
            O("dve", "tensor_copy", self.HC[:, l, 0:4], self.PS[pb][0:64, 0:4], r=[("ps", pb)], w=[("HC", l, 0)])
            O("dve", "tensor_scalar", self.HC[:, l, 4:7], self.HC[:, l, 0:3], self.HC[:, l, 3:4], None, ALU.mult,
              r=[("HC", l, 0)], w=[("HC", l, 1)])
            p.barrier(skip_q=("pool",))
        for l in range(self.NL):
            for nm, Lh in (("l", L), ("c", LC)):
                ntc = Lh // 128
                RT = self.av(0, [128, ntc, 3, 256], BF16)
                self.hyena_filter(l, nm, Lh, RT)
                DMA("sp", self.hf[nm][l, :, :, 0, :], RT[:, :, 0, :], r=[("RT", t, 0) for t in range(ntc)], w=[("hf", nm, l, 0)])
                DMA("sp", self.hf[nm][l, :, :, 1, :], RT[:, :, 2, :], r=[("RT", t, 2) for t in range(ntc)], w=[("hf", nm, l, 1)])
                p.barrier(skip_q=("pool",))
        DR = self.av(0, [1, 64], F32)
        DMA("sp", DR[0:1, 0:16], self.d["ret_decay_fwd"].rearrange("(o a) b -> o (a b)", o=1), w=[("DR", 0)])
        DMA("sp", DR[0:1, 16:32], self.d["ret_decay_bwd"].rearrange("(o a) b -> o (a b)", o=1), w=[("DR", 1)])
        DMA("sp", DR[0:1, 32:64], self.d["attn_sink"].rearrange("(o a) b -> o (a b)", o=1), w=[("DR", 2)])
        pb = self.psb()
        O("pe", "matmul", self.PS[pb][:, 0:64], self.ones_row[0:1, :], DR[0:1, :], start=True, stop=True,
          r=[("DR", 0), ("DR", 1), ("DR", 2), ("ones_row",)], w=[("ps", pb)])
        ET = self.av(256, [128, 32], F32)
        O("act", "activation", ET[:, :], self.PS[pb][:, 0:32], AF.Exp, r=[("ps", pb)], w=[("ET",)])
        O("act", "activation", self.LOGG[:, :], ET[:, :], AF.Ln, bias=1.0, scale=-1.0, r=[("ET",)], w=[("LOGG",)])
        O("act", "activation", self.SINKE[:, :], self.PS[pb][:, 32:64], AF.Exp, bias=-ATT_M, scale=1.0, r=[("ps", pb)], w=[("SINKE",)])
        p.barrier()

    def load_x(self, b):
        p, O, DMA = self.p, self.O, self.DMA
        for t in range(NT):
            st = self.av((t % 2) * 4096, [128, 1024], F32)
            src = self.d["x"][b, t * 128:(t + 1) * 128, :] if t < 16 else self.d["ctx"][b, (t - 16) * 128:(t - 15) * 128, :]
            DMA("sp", st[:, :], src, w=[("xst", t % 2)])
            for half in range(2):
                pb = self.psb()
                for q in range(4):
                    kc = half * 4 + q
                    O("pe", "transpose", self.PS[pb][:, q * 128:(q + 1) * 128], st[:, kc * 128:(kc + 1) * 128], self.ident_f[:, :],
                      r=[("xst", t % 2), ("ident_f",)], w=[("ps", pb)])
                dst = self.X[:, half * 4:(half + 1) * 4, t * 128:(t + 1) * 128]
                srcp = self.PS[pb][:, :].rearrange("p (a b) -> p a b", a=4)
                wk = [("X", kc, t) for kc in range(half * 4, half * 4 + 4)]
                if half == 0:
                    O("act", "copy", dst, srcp, r=[("ps", pb)], w=wk)
                else:
                    O("dve", "tensor_copy", dst, srcp, r=[("ps", pb)], w=wk)
        p.barrier()

    def store_out(self, b):
        p, O, DMA = self.p, self.O, self.DMA
        p.barrier()
        for t in range(16):
            st = self.av((t % 2) * 4096, [128, 1024], F32)
            for half in range(2):
                pb = self.psb()
                for q in range(4):
                    kc = half * 4 + q
                    O("pe", "transpose", self.PS[pb][:, q * 128:(q + 1) * 128], self.X[:, kc, t * 128:(t + 1) * 128], self.ident_f[:, :],
                      r=[("X", kc, t), ("ident_f",)], w=[("ps", pb)])
                dst = st[:, half * 512:(half + 1) * 512]
                if half == 0:
                    O("act", "copy", dst, self.PS[pb][:, :], r=[("ps", pb)], w=[("ost", t % 2, half)])
                else:
                    O("dve", "tensor_copy", dst, self.PS[pb][:, :], r=[("ps", pb)], w=[("ost", t % 2, half)])
            DMA("sp", self.out_d[b, t * 128:(t + 1) * 128, :], st[:, :], r=[("ost", t % 2, 0), ("ost", t % 2, 1)], w=[("outd", b, t)])
        p.barrier()

    def dump_dbg(self):
        p = self.p
        p.barrier()
        self.DMA("sp", self.dbg_x[:, :], self.X[:, :, :].rearrange("p a b -> p (a b)"), w=[("dbgx",)])
        self.DMA("sp", self.dbg_big[:, :], self.BIG[:, :], w=[("dbgb",)])
        p.barrier()

    def prenorm(self, b, l, kA, kS, groups, dst, dst_t0, aoff, dkey="HT"):
        O = self.O
        SQ = self.av(aoff, [128, 8, 512], BF16)
        for gi, (t0, n) in enumerate(groups):
            RS = self.av(aoff + 8192 + (gi % 2) * 2048, [128, 512], F32)
            bcol = b if t0 < L else self.NB
            tl = tiles_of(t0, n)
            for kc in range(8):
                xin = self.X[:, kc, t0:t0 + n]
                rk = [("X", kc, t) for t in tl]
                if kc % 2 == 0:
                    O("act", "activation", SQ[:, kc, 0:n], xin, AF.Square, r=rk, w=[("SQ", kc)])
                else:
                    O("pool", "tensor_tensor", SQ[:, kc, 0:n], xin, xin, ALU.mult, r=rk, w=[("SQ", kc)])
            pb = self.psb()
            for kc in range(8):
                O("pe", "matmul", self.PS[pb][:, 0:n], self.ones_b[:, :], SQ[:, kc, 0:n], start=(kc == 0), stop=(kc == 7),
                  r=[("SQ", kc), ("ones_b",)], w=[("ps", pb)])
            O("act", "activation", RS[:, 0:n], self.PS[pb][:, 0:n], AF.Sqrt, bias=EPS, scale=1.0 / D, r=[("ps", pb)], w=[("RS", gi % 2)])
            O("dve", "reciprocal", RS[:, 0:n], RS[:, 0:n], r=[("RS", gi % 2)], w=[("RS", gi % 2)])
            for kc in range(8):
                TMP = self.av(aoff + 12288 + (kc % 2) * 2048, [128, 512], F32)
                a_col = self.PAR[:, l, kA, kc, bcol:bcol + 1]
                s_col = self.PAR[:, l, kS, kc, bcol:bcol + 1]
                xin = self.X[:, kc, t0:t0 + n]
                O("dve", "scalar_tensor_tensor", TMP[:, 0:n], xin, a_col, RS[:, 0:n], ALU.mult, ALU.mult,
                  r=[("X", kc, t) for t in tl] + [("RS", gi % 2), ("PAR", l, kA)], w=[("PNT", kc % 2)])
                o = dst[:, kc, t0 - dst_t0: t0 - dst_t0 + n]
                O("act", "activation", o, TMP[:, 0:n], AF.Identity, bias=s_col, scale=1.0,
                  r=[("PNT", kc % 2), ("PAR", l, kS)], w=[(dkey, kc, t) for t in tl])

    def layer(self, b, l):
        p = self.p
        p.barrier()
        self.prenorm(b, l, PA1, PS1, GROUPS, self.HT, 0, 0)
        if self.dbg == "prenorm":
            return
        p.barrier()
        if self.dbg in (None, "mix", "wout", "full", "ret"):
            self.retention(b, l)
        p.barrier()
        if self.dbg in (None, "mix", "wout", "full", "hy"):
            self.hyena(b, l)
        p.barrier()
        if self.dbg in (None, "mix", "wout", "full", "att"):
            self.attention(b, l)
        p.barrier()
        if self.dbg in ("mix", "ret", "hy", "att"):
            return
        self.wout(b, l)
        p.barrier()
        if self.dbg == "wout":
            return
        self.ffn(b, l)
        p.barrier()

    def load_win(self, l, slab, c0, ncols, key):
        self.DMA("sp", slab, self.wb["w_in"][l, :, c0:c0 + ncols].rearrange("(kc p) n -> p kc n", p=128), w=[key])

    def proj_fm(self, lhs_slab, col_off, t0, n, key):
        pb = self.psb()
        tl = tiles_of(t0, n)
        for kc in range(8):
            self.O("pe", "matmul", self.PS[pb][:, 0:n], lhs_slab[:, kc, col_off:col_off + 128], self.HT[:, kc, t0:t0 + n],
                   start=(kc == 0), stop=(kc == 7), r=[key] + [("HT", kc, t) for t in tl], w=[("ps", pb)])
        return pb

    def proj_tm(self, rhs_slab, col_off, ncols, tile, key):
        pb = self.psb()
        for kc in range(8):
            self.O("pe", "matmul", self.PS[pb][:, 0:ncols], self.HT[:, kc, tile * 128:(tile + 1) * 128], rhs_slab[:, kc, col_off:col_off + ncols],
                   start=(kc == 0), stop=(kc == 7), r=[key, ("HT", kc, tile)], w=[("ps", pb)])
        return pb

    def rot_bufs(self, base_chunk, ntab):
        base = (8 + base_chunk) * TOK
        def fv(i):
            assert base + (i + 1) * 1024 <= 36864
            return self.BIG[:, base + i * 1024: base + (i + 1) * 1024].bitcast(F32)
        return {"QF": [fv(0), fv(1)], "T1": [fv(2), fv(3)], "TAB": [[fv(4 + s_ * ntab + k) for k in range(ntab)] for s_ in range(2)], "i": 0}

    def rotary(self, pb, n, dst, dkeys, RB, COS, SIN, ckeys):
        O = self.O
        si = RB["i"] % 2
        RB["i"] += 1
        QF, T1 = RB["QF"][si], RB["T1"][si]
        kq, kt = ("QF", si), ("T1", si)
        O("act", "copy", QF[:, 0:n], self.PS[pb][:, 0:n], r=[("ps", pb)], w=[kq])
        pb2 = self.psb()
        O("pe", "matmul", self.PS[pb2][:, 0:n], self.prot_f[:, :], QF[:, 0:n], start=True, stop=True, r=[kq, ("prot_f",)], w=[("ps", pb2)])
        O("dve", "tensor_tensor", T1[:, 0:n], QF[:, 0:n], COS[:, 0:n], ALU.mult, r=[kq] + ckeys, w=[kt])
        O("dve", "tensor_tensor", QF[:, 0:n], self.PS[pb2][:, 0:n], SIN[:, 0:n], ALU.mult, r=[("ps", pb2)] + ckeys, w=[kq])
        if isinstance(dst, list):
            for (p0, p1, d) in dst:
                O("dve", "tensor_tensor", d, T1[p0:p1, 0:n], QF[p0:p1, 0:n], ALU.add, r=[kt, kq], w=dkeys)
        else:
            O("dve", "tensor_tensor", dst, T1[:, 0:n], QF[:, 0:n], ALU.add, r=[kt, kq], w=dkeys)

    def retention(self, b, l):
        p, O, DMA = self.p, self.O, self.DMA
        RQT = self.av(0, [128, TOK], BF16)
        RKT = self.av(4608, [128, TOK], BF16)
        RV = self.av(9216, [128, NT, 256], BF16)
        RG = self.av(18432, [128, NT, 256], BF16)
        SF = self.av(27648, [128, NT, 256], BF16)
        SB = self.av(36864, [128, NT, 256], BF16)
        DMT = self.av(46080, [128, 4, 128], F32)
        QDF = self.av(48128, [128, 128], F32)
        QDB = self.av(48640, [128, 128], F32)
        KD4 = self.av(49152, [128, 8], F32)
        GC = self.av(49184, [128, 2], F32)
        LFM = self.av(49192, [128, 2], F32)
        LT4 = self.av(49200, [128, 4], F32)
        IT = self.av(27648, [128, 6, 128], F32)
        for i, nm in enumerate(("posd", "negd", "ge", "le", "ip1", "cmi")):
            DMA("sp", IT[:, i, :], self.k[nm][:, :], w=[("IT", i)])
        TA = self.av(27648 + 3072, [128, 128], F32)
        TB = self.av(27648 + 3584, [128, 128], F32)
        lgf = lambda h: self.LOGG[:, l * 4 + h: l * 4 + h + 1]
        lgb = lambda h: self.LOGG[:, 16 + l * 4 + h: 16 + l * 4 + h + 1]
        for h in range(4):
            O("act", "activation", TA[:, :], IT[:, 0, :], AF.Exp, scale=lgf(h), r=[("IT", 0), ("LOGG",)], w=[("TA",)])
            O("dve", "tensor_tensor", DMT[:, h, :], TA[:, :], IT[:, 2, :], ALU.mult, r=[("TA",), ("IT", 2)], w=[("DMT", h)])
            O("act", "activation", TB[:, :], IT[:, 1, :], AF.Exp, scale=lgb(h), r=[("IT", 1), ("LOGG",)], w=[("TB",)])
            O("dve", "tensor_tensor", TB[:, :], TB[:, :], IT[:, 3, :], ALU.mult, r=[("TB",), ("IT", 3)], w=[("TB",)])
            O("dve", "tensor_tensor", DMT[:, h, :], DMT[:, h, :], TB[:, :], ALU.add, r=[("TB",), ("DMT", h)], w=[("DMT", h)])
        for d in range(2):
            O("dve", "tensor_tensor", LT4[:, :], self.LOGG[:, d * 16 + l * 4: d * 16 + l * 4 + 4], self.hm[:, :], ALU.mult,
              r=[("LOGG",), ("hm",)], w=[("LT4",)])
            O("dve", "reduce_sum", LFM[:, d:d + 1], LT4[:, :], AX.X, r=[("LT4",)], w=[("LFM", d)])
            O("act", "activation", GC[:, d:d + 1], LFM[:, d:d + 1], AF.Exp, scale=128.0, r=[("LFM", d)], w=[("GC", d)])
        O("act", "activation", QDF[:, :], IT[:, 4, :], AF.Exp, scale=LFM[:, 0:1], r=[("IT", 4), ("LFM", 0)], w=[("QDF",)])
        O("act", "activation", QDB[:, :], IT[:, 5, :], AF.Exp, scale=LFM[:, 1:2], r=[("IT", 5), ("LFM", 1)], w=[("QDB",)])
        O("act", "activation", KD4[:, 0:4], self.LOGG[:, l * 4: l * 4 + 4], AF.Exp, scale=self.cols[:, 0:1], r=[("LOGG",), ("cols",)], w=[("KD4", 0)])
        O("act", "activation", KD4[:, 4:8], self.LOGG[:, 16 + l * 4: 16 + l * 4 + 4], AF.Exp, scale=self.cols[:, 1:2], r=[("LOGG",), ("cols",)], w=[("KD4", 1)])
        p.barrier()
        S0 = self.av(27648, [128, 8, 256], BF16)
        S1 = self.av(27648 + 4096, [128, 8, 256], BF16)
        S2 = self.av(27648 + 8192, [128, 8, 256], BF16)
        RBF = self.rot_bufs(2, 4)
        self.load_win(l, S0, 0, 256, ("S0",))
        self.load_win(l, S1, 256, 256, ("S1",))
        self.load_win(l, S2, 512, 256, ("S2",))
        for gi, (t0, n) in enumerate(GROUPS):
            TAB = RBF["TAB"][gi % 2]
            for k in range(4):
                DMA("sp", TAB[k][:, 0:n], self.k["rtab"][k, :, t0:t0 + n], w=[("RTAB", gi % 2, k)])
            for which in range(2):
                dstT = RQT if which == 0 else RKT
                pb = self.proj_fm(S0, which * 128, t0, n, ("S0",))
                self.rotary(pb, n, dstT[:, t0:t0 + n], [("RQK", which, t) for t in tiles_of(t0, n)], RBF, TAB[2 * which], TAB[2 * which + 1],
                            [("RTAB", gi % 2, 2 * which), ("RTAB", gi % 2, 2 * which + 1)])
        for t in range(NT):
            pbv = self.proj_tm(S1, 0, 256, t, ("S1",))
            O("act", "copy", RV[:, t, :], self.PS[pbv][:, 0:256], r=[("ps", pbv)], w=[("RV", t)])
            pbg = self.proj_tm(S2, 0, 256, t, ("S2",))
            O("act", "activation", RG[:, t, :], self.PS[pbg][:, 0:256], AF.Silu, r=[("ps", pbg)], w=[("RG", t)])
        p.barrier()
        KDb = [self.bigv(0 + d * 256, [128, 4, 32], BF16) for d in range(2)]
        KVM = [self.bigv(512 + d * 1024, [128, 256], F32) for d in range(2)]
        ST = self.bigv(2560, [128, 2, 256], F32)
        KMb = [self.bigv(4608 + i * 1024, [128, 4, 128], BF16) for i in range(2)]
        PTb = [self.bigv(6656 + i * 1024, [128, 512], BF16) for i in range(2)]
        QDb = [self.bigv(8704 + i * 512, [128, 2, 128], BF16) for i in range(2)]
        Rb = [self.bigv(9728 + i * 1024, [128, 256], F32) for i in range(2)]
        RBb = [self.bigv(11776 + i * 512, [128, 256], BF16) for i in range(2)]
        SSb = [self.bigv(12800 + i * 32, [128, 8], F32) for i in range(2)]
        SS2b = [self.bigv(12864 + i * 32, [128, 8], F32) for i in range(2)]

        def kv_step(n, d, first, save_to):
            KD, R = KDb[d], KVM[d]
            pbt = self.psb()
            ktok = self.PS[pbt][:, 0:64].bitcast(BF16)
            O("pe", "transpose", ktok, RKT[:, n * 128:(n + 1) * 128], self.ident_b[:, :], r=[("RQK", 1, n), ("ident_b",)], w=[("ps", pbt)])
            kdc = KD4[:, d * 4: d * 4 + 4].unsqueeze(2).to_broadcast([128, 4, 32])
            O("dve", "tensor_tensor", KD[:, :, :], ktok.rearrange("p (h k) -> p h k", h=4), kdc, ALU.mult,
              r=[("ps", pbt), ("KD4", d)], w=[("KD", d)])
            pbk = self.psb()
            O("pe", "matmul", self.PS[pbk][:, 0:256], KD[:, :, :].rearrange("p h k -> p (h k)"), RV[:, n, :], start=True, stop=True,
              r=[("KD", d), ("RV", n)], w=[("ps", pbk)])
            if first:
                O("dve", "tensor_tensor", ST[:, d, :], self.PS[pbk][:, 0:256], self.bd[:, :], ALU.mult, r=[("ps", pbk), ("bd",)], w=[("ST", d)])
            else:
                O("dve", "tensor_tensor", R[:, :], self.PS[pbk][:, 0:256], self.bd[:, :], ALU.mult, r=[("ps", pbk), ("bd",)], w=[("KVM", d)])
                O("dve", "scalar_tensor_tensor", ST[:, d, :], ST[:, d, :], GC[:, d:d + 1], R[:, :], ALU.mult, ALU.add,
                  r=[("KVM", d), ("ST", d), ("GC", d)], w=[("ST", d)])
            dstS, idx = save_to
            O("act", "copy", dstS[:, idx, :], ST[:, d, :], r=[("ST", d)], w=[("SFB", d, idx)])

        fwd = [(16, True, (SF, 17)), (17, False, (SF, 0))] + [(n, False, (SF, n + 1)) for n in range(15)]
        bwd = [(17, True, (SB, 16)), (16, False, (SB, 15))] + [(n, False, (SB, n - 1)) for n in range(15, 0, -1)]
        for (fa, ba) in zip(fwd, bwd):
            kv_step(fa[0], 0, fa[1], fa[2])
            kv_step(ba[0], 1, ba[1], ba[2])
        for n in range(NT):
            q = n % 2
            KM, PT, QD, R, RB, SS, SS2 = KMb[q], PTb[q], QDb[q], Rb[q], RBb[q], SSb[q], SS2b[q]
            for h in range(4):
                O("act", "activation", KM[:, h, :], RKT[:, n * 128:(n + 1) * 128], AF.Identity, scale=self.hm[:, h:h + 1],
                  r=[("RQK", 1, n), ("hm",)], w=[("KM", q, h)])
            pbs = self.psb()
            for h in range(4):
                O("pe", "matmul", self.PS[pbs][:, h * 128:(h + 1) * 128], KM[:, h, :], RQT[:, n * 128:(n + 1) * 128], start=True, stop=True,
                  r=[("KM", q, h), ("RQK", 0, n)], w=[("ps", pbs)])
            O("dve", "tensor_tensor", PT[:, :], self.PS[pbs][:, :], DMT[:, :, :].rearrange("p h i -> p (h i)"), ALU.mult,
              r=[("ps", pbs)] + [("DMT", h) for h in range(4)], w=[("PT", q)])
            O("pool", "tensor_tensor", QD[:, 0, :], RQT[:, n * 128:(n + 1) * 128], QDF[:, :], ALU.mult, r=[("RQK", 0, n), ("QDF",)], w=[("QD", q, 0)])
            O("pool", "tensor_tensor", QD[:, 1, :], RQT[:, n * 128:(n + 1) * 128], QDB[:, :], ALU.mult, r=[("RQK", 0, n), ("QDB",)], w=[("QD", q, 1)])
            pbo = self.psb()
            if n < 16:
                states = [(0, SF, n), (1, SB, n)]
            elif n == 16:
                states = [(1, SB, 16)]
            else:
                states = [(0, SF, 17)]
            first = True
            for (d, Sx, idx) in states:
                O("pe", "matmul", self.PS[pbo][:, 0:256], QD[:, d, :], Sx[:, idx, :], start=first, stop=False,
                  r=[("QD", q, d), ("SFB", d, idx)], w=[("ps", pbo)])
                first = False
            for h in range(4):
                O("pe", "matmul", self.PS[pbo][:, h * 64:(h + 1) * 64], PT[:, h * 128:(h + 1) * 128], RV[:, n, h * 64:(h + 1) * 64],
                  start=False, stop=(h == 3), r=[("PT", q), ("RV", n)], w=[("ps", pbo)])
            kR, kRB = ("R", q), ("RB", q)
            ps3 = self.PS[pbo][:, 0:256].rearrange("p (h d) -> p h d", h=4)
            O("act", "activation", R[:, :], self.PS[pbo][:, 0:256], AF.Square, r=[("ps", pbo)], w=[kR])
            O("dve", "reduce_sum", SS[:, 0:4], R[:, :].rearrange("p (h d) -> p h d", h=4), AX.X, r=[kR], w=[("SS", q, 0)])
            O("act", "activation", SS2[:, 0:4], SS[:, 0:4], AF.Sqrt, bias=EPS, scale=1.0 / 64, r=[("SS", q, 0)], w=[("SS2", q, 0)])
            O("dve", "reciprocal", SS2[:, 0:4], SS2[:, 0:4], r=[("SS2", q, 0)], w=[("SS2", q, 0)])
            O("dve", "tensor_tensor", R[:, :].rearrange("p (h d) -> p h d", h=4), ps3, SS2[:, 0:4].unsqueeze(2).to_broadcast([128, 4, 64]), ALU.mult,
              r=[("ps", pbo), ("SS2", q, 0), kR], w=[kR])
            O("pool", "tensor_tensor", R[:, :], R[:, :], RG[:, n, :], ALU.mult, r=[kR, ("RG", n)], w=[kR])
            O("act", "activation", RB[:, :], R[:, :], AF.Square, r=[kR], w=[kRB])
            O("dve", "reduce_sum", SS[:, 4:5], RB[:, :], AX.X, r=[kRB], w=[("SS", q, 1)])
            O("act", "activation", SS2[:, 4:5], SS[:, 4:5], AF.Sqrt, bias=EPS, scale=1.0 / 256, r=[("SS", q, 1)], w=[("SS2", q, 1)])
            O("dve", "reciprocal", SS2[:, 4:5], SS2[:, 4:5], r=[("SS2", q, 1)], w=[("SS2", q, 1)])
            O("dve", "tensor_scalar", RB[:, :], R[:, :], SS2[:, 4:5], None, ALU.mult, r=[kR, ("SS2", q, 1), kRB], w=[kRB])
            pbt = self.psb()
            tps = self.PS[pbt][:, 0:128].bitcast(BF16).rearrange("p (c t) -> p c t", c=2)
            for c in range(2):
                O("pe", "transpose", tps[:, c, :], RB[:, c * 128:(c + 1) * 128], self.ident_b[:, :], r=[kRB, ("ident_b",)], w=[("ps", pbt)])
            for c in range(2):
                O("dve", "tensor_scalar", self.MIXT[:, c, n * 128:(n + 1) * 128], tps[:, c, :], self.COLS[:, l, 32 + c:33 + c], None, ALU.mult,
                  r=[("ps", pbt), ("COLS", l)], w=[("MIXT", c, n)])

    def hyena(self, b, l):
        p, O, DMA = self.p, self.O, self.DMA
        RL = TOK + 4
        LAT0, CTX0 = 1, L + 3
        RAW = self.av(0, [128, 3, RL], F32)
        SL = self.av(27696, [128, 8, 3, 128], BF16)
        T1 = self.av(33840, [128, TOK], F32)
        T2 = self.av(43056, [128, TOK], F32)
        Z = self.MIXT[:, 4:6, :]
        X0C = self.MIXT[:, 6:8, :]
        for j in range(3):
            for c0 in (0, L + 1, L + 2, TOK + 3):
                O("pool", "memset", RAW[:, j, c0:c0 + 1], 0.0, w=[("RAWPAD", j)])
        for c in range(2):
            for j in range(3):
                col = 768 + j * 256 + c * 128
                DMA("sp", SL[:, :, j, :], self.wb["w_in"][l, :, col:col + 128].rearrange("(kc p) n -> p kc n", p=128), w=[("HSL", j)])
            for j in range(3):
                for (t0, n) in GROUPS:
                    pb = self.psb()
                    tl = tiles_of(t0, n)
                    for kc in range(8):
                        O("pe", "matmul", self.PS[pb][:, 0:n], SL[:, kc, j, :], self.HT[:, kc, t0:t0 + n], start=(kc == 0), stop=(kc == 7),
                          r=[("HSL", j)] + [("HT", kc, t) for t in tl], w=[("ps", pb)])
                    off = (LAT0 + t0) if t0 < L else (CTX0 + t0 - L)
                    O("act", "copy", RAW[:, j, off:off + n], self.PS[pb][:, 0:n], r=[("ps", pb)], w=[("RAW", j, t0)])
            rawkeys = lambda j: [("RAW", j, t0) for (t0, _) in GROUPS] + [("RAWPAD", j)]
            for j, dstT in ((0, T1), (1, T2)):
                ch = j * 2 + c
                w0 = self.COLS[:, l, 40 + 0 * 6 + ch: 41 + 0 * 6 + ch]
                w1 = self.COLS[:, l, 40 + 1 * 6 + ch: 41 + 1 * 6 + ch]
                w2 = self.COLS[:, l, 40 + 2 * 6 + ch: 41 + 2 * 6 + ch]
                bb = self.COLS[:, l, 58 + ch: 59 + ch]
                for (s0, d0, n) in ((LAT0, 0, L), (CTX0, L, LC)):
                    O("act", "activation", dstT[:, d0:d0 + n], RAW[:, j, s0:s0 + n], AF.Identity, bias=bb, scale=w1,
                      r=rawkeys(j) + [("COLS", l)], w=[("HT1", j, d0)])
                    O("dve", "scalar_tensor_tensor", dstT[:, d0:d0 + n], RAW[:, j, s0 - 1:s0 - 1 + n], w0, dstT[:, d0:d0 + n], ALU.mult, ALU.add,
                      r=rawkeys(j) + [("HT1", j, d0)], w=[("HT1", j, d0)])
                    O("dve", "scalar_tensor_tensor", dstT[:, d0:d0 + n], RAW[:, j, s0 + 1:s0 + 1 + n], w2, dstT[:, d0:d0 + n], ALU.mult, ALU.add,
                      r=rawkeys(j) + [("HT1", j, d0)], w=[("HT1", j, d0)])
            for (d0, n) in ((0, L), (L, LC)):
                O("pool", "tensor_tensor", Z[:, c, d0:d0 + n], T1[:, d0:d0 + n], T2[:, d0:d0 + n], ALU.mult,
                  r=[("HT1", 0, d0), ("HT1", 1, d0)], w=[("Z", c, d0)])
            j = 2
            ch = 4 + c
            w0 = self.COLS[:, l, 40 + 0 * 6 + ch: 41 + 0 * 6 + ch]
            w1 = self.COLS[:, l, 40 + 1 * 6 + ch: 41 + 1 * 6 + ch]
            w2 = self.COLS[:, l, 40 + 2 * 6 + ch: 41 + 2 * 6 + ch]
            bb = self.COLS[:, l, 58 + ch: 59 + ch]
            for (s0, d0, n) in ((LAT0, 0, L), (CTX0, L, LC)):
                O("act", "activation", T1[:, d0:d0 + n], RAW[:, j, s0:s0 + n], AF.Identity, bias=bb, scale=w1,
                  r=rawkeys(j) + [("COLS", l), ("Z", c, d0)], w=[("HT1", 0, d0)])
                O("dve", "scalar_tensor_tensor", T1[:, d0:d0 + n], RAW[:, j, s0 - 1:s0 - 1 + n], w0, T1[:, d0:d0 + n], ALU.mult, ALU.add,
                  r=rawkeys(j) + [("HT1", 0, d0)], w=[("HT1", 0, d0)])
                O("dve", "scalar_tensor_tensor", X0C[:, c, d0:d0 + n], RAW[:, j, s0 + 1:s0 + 1 + n], w2, T1[:, d0:d0 + n], ALU.mult, ALU.add,
                  r=rawkeys(j) + [("HT1", 0, d0)], w=[("X0C", c, d0)])
        p.barrier()
        self.hyena_seq(b, l, "l", L, 0)
        p.barrier()
        self.hyena_seq(b, l, "c", LC, L)

    def hyena_seq(self, b, l, nm, Lh, tok0):
        p, O, DMA = self.p, self.O, self.DMA
        ntc = Lh // 128
        nk = ntc
        tb = min(512, Lh)
        ntb = Lh // tb
        Z = self.MIXT[:, 4:6, :]
        X0C = self.MIXT[:, 6:8, :]
        RT = self.av(0, [128, ntc, 3, 256], BF16)
        DMA("sp", RT[:, :, 0, :], self.hf[nm][l, :, :, 0, :], w=[("RT", t, 0) for t in range(ntc)])
        DMA("sp", RT[:, :, 2, :], self.hf[nm][l, :, :, 1, :], w=[("RT", t, 2) for t in range(ntc)])
        for tcg in range(ntc):
            pb = self.psb()
            tp = self.PS[pb][:, 0:128].bitcast(BF16).rearrange("p (c k) -> p c k", c=2)
            for c in range(2):
                O("pe", "transpose", tp[:, c, :], Z[:, c, tok0 + tcg * 128: tok0 + (tcg + 1) * 128], self.ident_b[:, :],
                  r=[("Z", c, tok0), ("ident_b",)], w=[("ps", pb)])
            O("dve", "tensor_copy", RT[:, tcg, 1, :], self.PS[pb][:, 0:128].bitcast(BF16), r=[("ps", pb)], w=[("RT", tcg, 1)])
        Y = self.av(24576, [128, nk, 2, 256], BF16)
        FS = [self.av(40960 + i * 4096, [128, ntc, 128], BF16) for i in range(2)]
        G = self.av(49152, [128, 2, 256], F32)
        TT = self.av(51200, [128, 2, 256], F32)
        allrt = [("RT", t, j) for t in range(ntc) for j in range(3)]
        for kc in range(nk):
            pbs = []
            for cs in range(2):
                DMA("sp", FS[cs][:, :, :], self.k["dftf_" + nm][kc, cs, :, :, :], w=[("FS", cs)])
                pb = self.psb()
                pbs.append(pb)
                for tc in range(ntc):
                    rhs = RT[:, tc, cs:cs + 2, :].rearrange("p a c -> p (a c)")
                    O("pe", "matmul", self.PS[pb][:, :], FS[cs][:, tc, :], rhs, start=(tc == 0), stop=(tc == ntc - 1),
                      r=[("FS", cs)] + (allrt if tc == 0 else []), w=[("ps", pb)])
            pc, ps_ = pbs
            O("act", "copy", G[:, 0, :], self.PS[pc][:, 0:256], r=[("ps", pc)], w=[("G", 0)])
            O("act", "copy", G[:, 1, :], self.PS[ps_][:, 256:512], r=[("ps", ps_)], w=[("G", 1)])
            Zr = self.PS[pc][:, 256:512]
            Zs = self.PS[ps_][:, 0:256]
            O("dve", "tensor_tensor", TT[:, 0, :], Zr, G[:, 0, :], ALU.mult, r=[("ps", pc), ("G", 0)], w=[("TT", 0)])
            O("dve", "tensor_tensor", TT[:, 1, :], Zs, G[:, 1, :], ALU.mult, r=[("ps", ps_), ("G", 1)], w=[("TT", 1)])
            O("pool", "tensor_tensor", Y[:, kc, 0, :], TT[:, 0, :], TT[:, 1, :], ALU.add, r=[("TT", 0), ("TT", 1)], w=[("Y", kc, 0)])
            O("dve", "tensor_tensor", TT[:, 0, :], Zr, G[:, 1, :], ALU.mult, r=[("ps", pc), ("G", 1), ("Y", kc, 0)], w=[("TT", 0)])
            O("dve", "tensor_tensor", TT[:, 1, :], Zs, G[:, 0, :], ALU.mult, r=[("ps", ps_), ("G", 0), ("Y", kc, 0)], w=[("TT", 1)])
            O("pool", "tensor_tensor", Y[:, kc, 1, :], TT[:, 0, :], TT[:, 1, :], ALU.subtract, r=[("TT", 0), ("TT", 1)], w=[("Y", kc, 1)])
        p.barrier()
        nks = max(1, nk // 4)
        kper = nk // nks
        IS = [self.av(i * 8192, [128, kper, 2, tb], BF16) for i in range(3)]
        HYO = self.av(40960, [128, 2, 512], F32)
        SQ = self.av(45056, [128, 2, 512], BF16)
        RS = self.av(47104, [128, 512], F32)
        T = self.av(49152, [128, 512], F32)
        si = 0
        ally = [("Y", kc, j) for kc in range(nk) for j in range(2)]
        for bi in range(ntb):
            t0 = tok0 + bi * tb
            pbs = [self.psb(), self.psb()]
            for ks in range(nks):
                sl = IS[si % 3]
                key = ("IS", si % 3)
                si += 1
                DMA("sp", sl[:, :, :, :], self.k["dfti_" + nm][bi, :, ks * kper:(ks + 1) * kper, :, :], w=[key])
                for c in range(2):
                    for kci in range(kper):
                        kc = ks * kper + kci
                        for ri in range(2):
                            first = (ks == 0 and kci == 0 and ri == 0)
                            last = (ks == nks - 1 and kci == kper - 1 and ri == 1)
                            O("pe", "matmul", self.PS[pbs[c]][:, 0:tb], Y[:, kc, ri, c * 128:(c + 1) * 128], sl[:, kci, ri, :], start=first, stop=last,
                              r=[key] + (ally if first else []), w=[("ps", pbs[c])])
            for c in range(2):
                O("dve", "scalar_tensor_tensor", T[:, 0:tb], Z[:, c, t0:t0 + tb], self.COLS[:, l, 64 + c:65 + c], self.PS[pbs[c]][:, 0:tb], ALU.mult, ALU.add,
                  r=[("ps", pbs[c]), ("Z", c, tok0), ("COLS", l)], w=[("T",)])
                O("dve", "tensor_tensor", HYO[:, c, 0:tb], T[:, 0:tb], X0C[:, c, t0:t0 + tb], ALU.mult, r=[("T",), ("X0C", c, tok0)], w=[("HYO", c)])
                O("act", "activation", SQ[:, c, 0:tb], HYO[:, c, 0:tb], AF.Square, r=[("HYO", c)], w=[("SQ", c)])
            pbn = self.psb()
            for c in range(2):
                O("pe", "matmul", self.PS[pbn][:, 0:tb], self.ones_b[:, :], SQ[:, c, 0:tb], start=(c == 0), stop=(c == 1),
                  r=[("SQ", c), ("ones_b",)], w=[("ps", pbn)])
            O("act", "activation", RS[:, 0:tb], self.PS[pbn][:, 0:tb], AF.Sqrt, bias=EPS, scale=1.0 / 256, r=[("ps", pbn)], w=[("RS",)])
            O("dve", "reciprocal", RS[:, 0:tb], RS[:, 0:tb], r=[("RS",)], w=[("RS",)])
            for c in range(2):
                O("dve", "scalar_tensor_tensor", self.MIXT[:, 2 + c, t0:t0 + tb], HYO[:, c, 0:tb], self.COLS[:, l, 34 + c:35 + c], RS[:, 0:tb], ALU.mult, ALU.mult,
                  r=[("HYO", c), ("RS",), ("COLS", l)], w=[("MIXT", 2 + c, t) for t in tiles_of(t0, tb)])

    def hyena_filter(self, l, nm, Lh, RT):
        p, O, DMA = self.p, self.O, self.DMA
        fo = 24576
        ZE = self.av(fo, [33, 512], F32)
        HA = self.av(fo + 2048, [64, 512], F32)
        HB = self.av(fo + 4096, [64, 512], F32)
        ARG = self.av(fo + 6144, [64, 512], F32)
        M1 = self.av(fo + 8192, [64, 512], F32)
        W1 = self.av(fo + 10240, [33, 64], F32)
        W2 = self.av(fo + 10496, [64, 2, 64], F32)
        W3 = self.av(fo + 11008, [64, 512], F32)
        DEC = self.av(fo + 13056, [128, 2, 256], F32)
        HF = self.av(fo + 15104, [128, 512], F32)
        DMA("sp", W1[:, :], self.d["hy_f_w1"][l], w=[("W1",)])
        DMA("sp", W2[:, :, :], self.d["hy_f_w2"][l].rearrange("j i o -> i j o"), w=[("W2",)])
        DMA("sp", W3[:, :], self.d["hy_f_w3"][l], w=[("W3",)])
        PI = math.pi
        for g0 in range(0, Lh, 512):
            n = min(512, Lh - g0)
            DMA("sp", ZE[:, 0:n], self.k["zembT_" + nm][:, g0:g0 + n], w=[("ZE",)])
            cur_in, cur_w, kdim = ZE, W1[:, :], 33
            hbuf = [HA, HB, HA]
            for li in range(3):
                pb = self.psb()
                O("pe", "matmul", self.PS[pb][0:64, 0:n], cur_w, cur_in[0:kdim, 0:n], start=True, stop=True,
                  r=[("ZE",), ("W1",), ("W2",), ("HH", 0), ("HH", 1)], w=[("ps", pb)])
                O("act", "activation", ARG[:, 0:n], self.PS[pb][0:64, 0:n], AF.Identity, bias=self.HC[:, l, 4 + li:5 + li], scale=self.HC[:, l, 3:4],
                  r=[("ps", pb), ("HC", l, 0), ("HC", l, 1)], w=[("ARG",)])
                for _ in range(2):
                    O("dve", "tensor_scalar", M1[:, 0:n], ARG[:, 0:n], PI, -2.0 * PI, ALU.is_gt, ALU.mult, r=[("ARG",)], w=[("M1",)])
                    O("dve", "tensor_tensor", ARG[:, 0:n], ARG[:, 0:n], M1[:, 0:n], ALU.add, r=[("ARG",), ("M1",)], w=[("ARG",)])
                    O("dve", "tensor_scalar", M1[:, 0:n], ARG[:, 0:n], -PI, 2.0 * PI, ALU.is_lt, ALU.mult, r=[("ARG",)], w=[("M1",)])
                    O("dve", "tensor_tensor", ARG[:, 0:n], ARG[:, 0:n], M1[:, 0:n], ALU.add, r=[("ARG",), ("M1",)], w=[("ARG",)])
                O("act", "activation", hbuf[li][:, 0:n], ARG[:, 0:n], AF.Sin, r=[("ARG",)], w=[("HH", li % 2)])
                cur_in, kdim = hbuf[li], 64
                if li < 2:
                    cur_w = W2[:, li, :]
            hfin = hbuf[2]
            for tt in range(n // 128):
                tcg = g0 // 128 + tt
                pb = self.psb()
                O("pe", "matmul", self.PS[pb][:, :], hfin[:, tt * 128:(tt + 1) * 128], W3[:, :], start=True, stop=True,
                  r=[("HH", 0), ("W3",)], w=[("ps", pb)])
                DMA("sp", DEC[:, tcg % 2, :], self.k["decay_" + nm][:, tcg, :], w=[("DEC", tcg % 2)])
                O("dve", "tensor_tensor", HF[:, :].rearrange("p (a c) -> p a c", a=2), self.PS[pb][:, :].rearrange("p (a c) -> p a c", a=2),
                  DEC[:, tcg % 2, :].unsqueeze(1).to_broadcast([128, 2, 256]), ALU.mult, r=[("ps", pb), ("DEC", tcg % 2)], w=[("HF",)])
                if tcg == 0:
                    O("dve", "tensor_scalar", HF[:, 256:512], HF[:, 256:512], self.cols[:, 2:3], None, ALU.mult, r=[("HF",), ("cols",)], w=[("HF",)])
                O("dve", "tensor_tensor", RT[:, tcg, 0, :], HF[:, 0:256], HF[:, 256:512], ALU.add, r=[("HF",)], w=[("RT", tcg, 0)])
                O("dve", "tensor_tensor", RT[:, tcg, 2, :], HF[:, 256:512], HF[:, 0:256], ALU.subtract, r=[("HF",)], w=[("RT", tcg, 2)])

    def attention(self, b, l):
        p, O, DMA = self.p, self.O, self.DMA
        AQT = self.av(0, [128, 4, TOK], BF16)
        AKZ = self.av(18432, [128, 4, TOK], BF16)
        VX = self.av(36864, [128, NT, 2, 66], BF16)
        so = 41632
        SAs = [self.av(so + i * 2048, [128, 8, 128], BF16) for i in range(7)]
        RBF = self.rot_bufs(4, 2)
        O("pool", "memset", VX[:, :, :, 64:65], 1.0, w=[("VX1",)])
        for g in range(2):
            O("pool", "memset", AKZ[64:128, g * 2 + 0, :], 0.0, w=[("AKZ0", g, 0)])
            O("pool", "memset", AKZ[0:64, g * 2 + 1, :], 0.0, w=[("AKZ0", g, 1)])
        for i in range(6):
            if i < 4:
                self.load_win(l, SAs[i], 1536 + i * 128, 128, ("SA", i))
            else:
                g = i - 4
                for hlf in range(2):
                    DMA("sp", SAs[i][:, :, hlf * 64:(hlf + 1) * 64],
                        self.wb["w_in"][l, :, 2048 + g * 64: 2048 + (g + 1) * 64].rearrange("(kc p) n -> p kc n", p=128), w=[("SA", i, hlf)])
        self.load_win(l, SAs[6], 2176, 128, ("SA", 6))
        for gi, (t0, n) in enumerate(GROUPS):
            COS, SIN = RBF["TAB"][gi % 2]
            DMA("sp", COS[:, 0:n], self.k["atab"][0, :, t0:t0 + n], w=[("ATAB", gi % 2, 0)])
            DMA("sp", SIN[:, 0:n], self.k["atab"][1, :, t0:t0 + n], w=[("ATAB", gi % 2, 1)])
            tl = tiles_of(t0, n)
            for i in range(6):
                pb = self.psb()
                rk = [("SA", i)] if i < 4 else [("SA", i, 0), ("SA", i, 1)]
                for kc in range(8):
                    O("pe", "matmul", self.PS[pb][:, 0:n], SAs[i][:, kc, :], self.HT[:, kc, t0:t0 + n], start=(kc == 0), stop=(kc == 7),
                      r=rk + [("HT", kc, t) for t in tl], w=[("ps", pb)])
                if i < 4:
                    dst = AQT[:, i, t0:t0 + n]
                else:
                    g = i - 4
                    dst = [(0, 64, AKZ[0:64, g * 2 + 0, t0:t0 + n]), (64, 128, AKZ[64:128, g * 2 + 1, t0:t0 + n])]
                self.rotary(pb, n, dst, [("AQK", i, t) for t in tl], RBF, COS, SIN, [("ATAB", gi % 2, 0), ("ATAB", gi % 2, 1)])
        for t in range(NT):
            pb = self.proj_tm(SAs[6], 0, 128, t, ("SA", 6))
            O("act", "copy", VX[:, t, :, 0:64], self.PS[pb][:, 0:128].rearrange("p (g d) -> p g d", g=2), r=[("ps", pb), ("VX1",)], w=[("VX", t)])
        p.barrier()
        NPB = 6
        PB = [self.av(so + i * 1024, [128, 512], BF16) for i in range(NPB)]
        ATT = [self.av(so + 6144 + i * 2048, [128, 512], F32) for i in range(2)]
        ATB = [self.av(so + 10240 + i * 1024, [128, 512], BF16) for i in range(2)]
        JNK = self.av(so + 12288, [128, 512], BF16)
        DEN = self.av(so + 13312, [128, 8], F32)
        SS = self.av(so + 13344, [128, 4], F32)
        jobs = []
        for n in range(NT):
            if n < 16:
                kbs = ([(n - 1, self.mprev)] if n > 0 else []) + [(n, None)] + ([(n + 1, self.mnext)] if n < 15 else []) + [(16, None), (17, None)]
            else:
                kbs = [(16, None), (17, None)]
            for g in range(2):
                for ki, (kb, mask) in enumerate(kbs):
                    jobs.append((n, g, ki, len(kbs), kb, mask))
        LA = 2
        pend = {}
        accb = {}

        def unit_epilogue(n, g):
            pba = accb[(n, g)]
            acc = self.PS[pba][:, 0:264].rearrange("p (h d) -> p h d", h=4)
            att = ATT[n % 2]
            O("dve", "tensor_tensor", DEN[:, g * 4:(g + 1) * 4], acc[:, :, 64], self.SINKE[:, l * 8 + g * 4: l * 8 + g * 4 + 4], ALU.add,
              r=[("ps", pba), ("SINKE",)], w=[("DEN", g)])
            O("dve", "reciprocal", DEN[:, g * 4:(g + 1) * 4], DEN[:, g * 4:(g + 1) * 4], r=[("DEN", g)], w=[("DEN", g)])
            O("dve", "tensor_tensor", att[:, g * 256:(g + 1) * 256].rearrange("p (h d) -> p h d", h=4), acc[:, :, 0:64],
              DEN[:, g * 4:(g + 1) * 4].unsqueeze(2).to_broadcast([128, 4, 64]), ALU.mult, r=[("ps", pba), ("DEN", g)], w=[("ATT", n % 2, g)])

        def block_epilogue(n):
            att = ATT[n % 2]
            O("act", "activation", JNK[:, :], att[:, :], AF.Square, r=[("ATT", n % 2, 0), ("ATT", n % 2, 1)], w=[("JNK",)])
            O("dve", "reduce_sum", SS[:, 0:1], JNK[:, :], AX.X, r=[("JNK",)], w=[("SS", 0)])
            O("act", "activation", SS[:, 1:2], SS[:, 0:1], AF.Sqrt, bias=EPS, scale=1.0 / 512, r=[("SS", 0)], w=[("SS", 1)])
            O("dve", "reciprocal", SS[:, 1:2], SS[:, 1:2], r=[("SS", 1)], w=[("SS", 1)])
            atb = ATB[n % 2]
            O("dve", "tensor_scalar", atb[:, :], att[:, :], SS[:, 1:2], None, ALU.mult, r=[("ATT", n % 2, 0), ("ATT", n % 2, 1), ("SS", 1)], w=[("ATB", n % 2)])
            pbt = self.psb()
            tp = self.PS[pbt][:, 0:256].bitcast(BF16).rearrange("p (c t) -> p c t", c=4)
            for c in range(4):
                O("pe", "transpose", tp[:, c, :], atb[:, c * 128:(c + 1) * 128], self.ident_b[:, :], r=[("ATB", n % 2), ("ident_b",)], w=[("ps", pbt)])
            for c in range(4):
                O("dve", "tensor_scalar", self.MIXT[:, 4 + c, n * 128:(n + 1) * 128], tp[:, c, :], self.COLS[:, l, 36 + c:37 + c], None, ALU.mult,
                  r=[("ps", pbt), ("COLS", l)], w=[("MIXT", 4 + c, n)])

        for k in range(len(jobs) + LA):
            if k < len(jobs):
                (n, g, ki, nkb, kb, mask) = jobs[k]
                pbs = self.psb()
                for hh in range(4):
                    head = 4 * g + hh
                    c, hf = head // 2, head % 2
                    O("pe", "matmul", self.PS[pbs][:, hh * 128:(hh + 1) * 128], AKZ[:, g * 2 + hf, kb * 128:(kb + 1) * 128],
                      AQT[:, c, n * 128:(n + 1) * 128], start=True, stop=True,
                      r=[("AQK", 4 + g, kb), ("AQK", c, n), ("AKZ0", g, 0), ("AKZ0", g, 1)], w=[("ps", pbs)])
                P = PB[k % NPB]
                pk = ("P", k % NPB)
                O("act", "activation", P[:, :], self.PS[pbs][:, :], AF.Exp, bias=-ATT_M, scale=0.125, r=[("ps", pbs)], w=[pk])
                if mask is not None:
                    O("dve", "tensor_tensor", P[:, :].rearrange("p (h i) -> p h i", h=4), P[:, :].rearrange("p (h i) -> p h i", h=4),
                      mask[:, :].unsqueeze(1).to_broadcast([128, 4, 128]), ALU.mult, r=[pk, ("mprev",), ("mnext",)], w=[pk])
                pend[k] = (P, pk)
            j = k - LA
            if j >= 0:
                (n, g, ki, nkb, kb, mask) = jobs[j]
                P, pk = pend.pop(j)
                if ki == 0:
                    accb[(n, g)] = self.psb()
                pba = accb[(n, g)]
                acc = self.PS[pba][:, 0:264].rearrange("p (h d) -> p h d", h=4)
                for hh in range(4):
                    O("pe", "matmul", acc[:, hh, 0:65], P[:, hh * 128:(hh + 1) * 128], VX[:, kb, g, 0:65],
                      start=(ki == 0 and hh == 0), stop=(ki == nkb - 1 and hh == 3), skip_group_check=True,
                      r=[pk, ("VX", kb), ("VX1",)], w=[("ps", pba)])
                if ki == nkb - 1:
                    unit_epilogue(n, g)
                    if g == 1:
                        block_epilogue(n)

    def postnorm_resid(self, b, l, kG, TMPO, SQ, RS, Tt, t0, n, toff):
        O = self.O
        bcol = b if t0 < L else self.NB
        tl = tiles_of(t0, n)
        pb = self.psb()
        for oc in range(8):
            O("pe", "matmul", self.PS[pb][:, 0:n], self.ones_b[:, :], SQ[:, oc, toff:toff + n], start=(oc == 0), stop=(oc == 7),
              r=[("SQo", oc, toff), ("ones_b",)], w=[("ps", pb)])
        O("act", "activation", RS[:, 0:n], self.PS[pb][:, 0:n], AF.Sqrt, bias=EPS, scale=1.0 / D, r=[("ps", pb)], w=[("RSo",)])
        O("dve", "reciprocal", RS[:, 0:n], RS[:, 0:n], r=[("RSo",)], w=[("RSo",)])
        for oc in range(8):
            T = Tt[oc % 2]
            gcol = self.PAR[:, l, kG, oc, bcol:bcol + 1]
            O("dve", "scalar_tensor_tensor", T[:, 0:n], TMPO[:, oc, toff:toff + n], gcol, RS[:, 0:n], ALU.mult, ALU.mult,
              r=[("TMPO", oc, toff), ("RSo",), ("PAR", l, kG)], w=[("To", oc % 2)])
            xs = self.X[:, oc, t0:t0 + n]
            O("pool", "tensor_tensor", xs, xs, T[:, 0:n], ALU.add, r=[("To", oc % 2)] + [("X", oc, t) for t in tl], w=[("X", oc, t) for t in tl])

    def wout(self, b, l):
        p, O, DMA = self.p, self.O, self.DMA
        W = self.av(0, [128, 8, 1024], BF16)
        TMPO = self.av(16384, [128, 8, 512], F32)
        SQ = self.av(32768, [128, 8, 512], BF16)
        RS = self.av(40960, [128, 512], F32)
        Tt = [self.av(43008 + i * 2048, [128, 512], F32) for i in range(2)]
        for hlf in range(2):
            DMA("sp", W[:, :, hlf * 512:(hlf + 1) * 512], self.wb["w_out"][l, :, hlf * 512:(hlf + 1) * 512].rearrange("(kc p) n -> p kc n", p=128),
                w=[("W", hlf)])
        for (t0, n) in GROUPS:
            tl = tiles_of(t0, n)
            for oc in range(8):
                pb = self.psb()
                for kc in range(8):
                    O("pe", "matmul", self.PS[pb][:, 0:n], W[:, kc, oc * 128:(oc + 1) * 128], self.MIXT[:, kc, t0:t0 + n], start=(kc == 0), stop=(kc == 7),
                      r=[("W", oc // 4)] + [("MIXT", kc, t) for t in tl], w=[("ps", pb)])
                O("act", "copy", TMPO[:, oc, 0:n], self.PS[pb][:, 0:n], r=[("ps", pb)], w=[("TMPO", oc, 0)])
                O("dve", "tensor_tensor", SQ[:, oc, 0:n], self.PS[pb][:, 0:n], TMPO[:, oc, 0:n], ALU.mult, r=[("ps", pb), ("TMPO", oc, 0)], w=[("SQo", oc, 0)])
            self.postnorm_resid(b, l, PG1, TMPO, SQ, RS, Tt, t0, n, 0)

    def ffn(self, b, l):
        p, O, DMA = self.p, self.O, self.DMA
        AT = self.bv(0, [128, 32, 768])
        H2T = self.bv(24576, [128, 8, 768])
        SQF = self.bv(30720, [128, 8, 768])
        RING = [self.av(i * 8192, [128, 4096], BF16) for i in range(3)]
        TMPO = self.av(24576, [128, 8, 768], F32)
        RS = self.av(49152, [128, 512], F32)
        RL = [self.av(51200 + i * 2048, [128, 512], F32) for i in range(2)]
        ri = 0
        pref = {}
        for ps_i, groups in enumerate(FFN_PASSES):
            p.barrier()
            base_t0 = groups[0][0]
            self.prenorm_local(b, l, groups, H2T, base_t0, 24576)
            for s in range(8):
                if s in pref:
                    slab, key = pref.pop(s)
                else:
                    slab = RING[ri % 3].rearrange("p (k n) -> p k n", k=8)
                    key = ("RING", ri % 3)
                    ri += 1
                    DMA("sp", slab, self.wb["w_ff1"][l, :, s * 512:(s + 1) * 512].rearrange("(kc p) n -> p kc n", p=128), w=[key])
                for oc4 in range(4):
                    oc = s * 4 + oc4
                    for (t0, n) in groups:
                        lo = t0 - base_t0
                        pb = self.psb()
                        for kc in range(8):
                            O("pe", "matmul", self.PS[pb][:, 0:n], slab[:, kc, oc4 * 128:(oc4 + 1) * 128], H2T[:, kc, lo:lo + n], start=(kc == 0), stop=(kc == 7),
                              r=[key] + [("H2T", kc, t) for t in tiles_of(lo, n)], w=[("ps", pb)])
                        rl = RL[(oc + (lo > 0)) % 2]
                        rk = ("RL", (oc + (lo > 0)) % 2)
                        O("act", "activation", rl[:, 0:n], self.PS[pb][:, 0:n], AF.Relu, r=[("ps", pb)], w=[rk])
                        O("pool", "tensor_tensor", AT[:, oc, lo:lo + n], rl[:, 0:n], rl[:, 0:n], ALU.mult, r=[rk], w=[("AT", oc, lo)])
            for oc in range(8):
                slab = RING[ri % 3].rearrange("p (k n) -> p k n", k=32)
                key = ("RING", ri % 3)
                ri += 1
                DMA("sp", slab, self.wb["w_ff2"][l, :, oc * 128:(oc + 1) * 128].rearrange("(kc p) n -> p kc n", p=128), w=[key])
                for (t0, n) in groups:
                    lo = t0 - base_t0
                    pb = self.psb()
                    for kc in range(32):
                        O("pe", "matmul", self.PS[pb][:, 0:n], slab[:, kc, :], AT[:, kc, lo:lo + n], start=(kc == 0), stop=(kc == 31),
                          r=[key, ("AT", kc, lo)], w=[("ps", pb)])
                    O("act", "copy", TMPO[:, oc, lo:lo + n], self.PS[pb][:, 0:n], r=[("ps", pb)], w=[("TMPO", oc, lo)])
                    O("dve", "tensor_tensor", SQF[:, oc, lo:lo + n], self.PS[pb][:, 0:n], TMPO[:, oc, lo:lo + n], ALU.mult,
                      r=[("ps", pb), ("TMPO", oc, lo)], w=[("SQo", oc, lo)])
            if ps_i + 1 < len(FFN_PASSES):
                for s in range(2):
                    slab = RING[ri % 3].rearrange("p (k n) -> p k n", k=8)
                    key = ("RING", ri % 3)
                    ri += 1
                    DMA("sp", slab, self.wb["w_ff1"][l, :, s * 512:(s + 1) * 512].rearrange("(kc p) n -> p kc n", p=128), w=[key])
                    pref[s] = (slab, key)
            for (t0, n) in groups:
                self.postnorm_resid(b, l, PG2, TMPO, SQF, RS, RL, t0, n, t0 - base_t0)

    def prenorm_local(self, b, l, groups, H2T, base_t0, aoff):
        O = self.O
        SQ = self.av(aoff, [128, 8, 512], BF16)
        for gi, (t0, n) in enumerate(groups):
            RS = self.av(aoff + 8192 + (gi % 2) * 2048, [128, 512], F32)
            bcol = b if t0 < L else self.NB
            tl = tiles_of(t0, n)
            lo = t0 - base_t0
            for kc in range(8):
                xin = self.X[:, kc, t0:t0 + n]
                rk = [("X", kc, t) for t in tl]
                if kc % 2 == 0:
                    O("act", "activation", SQ[:, kc, 0:n], xin, AF.Square, r=rk, w=[("SQ", kc)])
                else:
                    O("pool", "tensor_tensor", SQ[:, kc, 0:n], xin, xin, ALU.mult, r=rk, w=[("SQ", kc)])
            pb = self.psb()
            for kc in range(8):
                O("pe", "matmul", self.PS[pb][:, 0:n], self.ones_b[:, :], SQ[:, kc, 0:n], start=(kc == 0), stop=(kc == 7),
                  r=[("SQ", kc), ("ones_b",)], w=[("ps", pb)])
            O("act", "activation", RS[:, 0:n], self.PS[pb][:, 0:n], AF.Sqrt, bias=EPS, scale=1.0 / D, r=[("ps", pb)], w=[("RS", gi % 2)])
            O("dve", "reciprocal", RS[:, 0:n], RS[:, 0:n], r=[("RS", gi % 2)], w=[("RS", gi % 2)])
            for kc in range(8):
                TMP = self.av(aoff + 12288 + (kc % 2) * 2048, [128, 512], F32)
                a_col = self.PAR[:, l, PA2, kc, bcol:bcol + 1]
                s_col = self.PAR[:, l, PS2, kc, bcol:bcol + 1]
                xin = self.X[:, kc, t0:t0 + n]
                O("dve", "scalar_tensor_tensor", TMP[:, 0:n], xin, a_col, RS[:, 0:n], ALU.mult, ALU.mult,
                  r=[("X", kc, t) for t in tl] + [("RS", gi % 2), ("PAR", l, PA2)], w=[("PNT", kc % 2)])
                O("act", "activation", H2T[:, kc, lo:lo + n], TMP[:, 0:n], AF.Identity, bias=s_col, scale=1.0,
                  r=[("PNT", kc % 2), ("PAR", l, PS2)], w=[("H2T", kc, t) for t in tiles_of(lo, n)])


_NC_CACHE = {}


def kernel(**inputs):
    NB = 4
    n_cores = 8
    key = (NB, DEPTH)
    if key not in _NC_CACHE:
        _NC_CACHE[key] = Builder(NB=NB, NL=DEPTH).build()
    nc = _NC_CACHE[key]
    consts = get_consts()
    in_maps = []
    for ci in range(n_cores):
        m = {}
        m["x"] = np.ascontiguousarray(inputs["x"][ci * NB:(ci + 1) * NB], dtype=np.float32)
        m["ctx"] = np.ascontiguousarray(inputs["ctx"][ci * NB:(ci + 1) * NB], dtype=np.float32)
        m["c"] = np.ascontiguousarray(inputs["c"][ci * NB:(ci + 1) * NB], dtype=np.float32)
        m["c_ctx"] = np.ascontiguousarray(np.asarray(inputs["c_ctx"], dtype=np.float32)[None, :])
        for name, _ in W_NAMES:
            m[name] = np.ascontiguousarray(inputs[name], dtype=np.float32)
        for name, arr in consts.items():
            m["k_" + name] = arr
        in_maps.append(m)
    res = run_bass_kernel_spmd(nc, in_maps, core_ids=list(range(n_cores)))
    out = np.concatenate([np.asarray(r["out"]) for r in res.results], axis=0)
    return out.astype(np.float32)
```

```python
import math
import numpy as np
import ml_dtypes
import concourse.bass as bass
import concourse.mybir as mybir
from concourse.bass_utils import run_bass_kernel_spmd

F32 = mybir.dt.float32
BF16 = mybir.dt.bfloat16
AF = mybir.ActivationFunctionType
ALU = mybir.AluOpType
AX = mybir.AxisListType

D = 1024
L = 2048
LC = 256
TOK = L + LC
NT = TOK // 128
DEPTH = 4
D_IN = 2304
D_FF = 4096
EPS = 1e-6
HY_CH = 256
ATT_M = 20.0
GROUPS = [(0, 512), (512, 512), (1024, 512), (1536, 512), (2048, 256)]
FFN_PASSES = [[(0, 512), (512, 256)], [(768, 512), (1280, 256)], [(1536, 512), (2048, 256)]]


def tiles_of(t0, n):
    return list(range(t0 // 128, (t0 + n + 127) // 128))


class Prog:
    ENGS = ("pe", "act", "dve", "pool", "sp")

    def __init__(self, nc, eng_sems, dma_sems):
        self.nc = nc
        self.eng_sem = eng_sems
        self.dma_sems = dma_sems
        self.dma_val = {q: [0] * len(v) for q, v in dma_sems.items()}
        self.dma_rr = {q: 0 for q in dma_sems}
        self.count = {e: 0 for e in self.ENGS}
        self.ops = {e: [] for e in self.ENGS}
        self.last_write = {}
        self.readers = {}
        self.waited = {}
        self.semid = {}
        self.n_ops = 0

    def _sid(self, sem):
        return id(sem)

    def _deps(self, reads, writes, eng=None):
        deps = []
        for k in reads:
            t = self.last_write.get(k)
            if t is not None:
                deps.append(t)
            if k[0] == "ps":
                deps.extend(tk for tk in self.readers.get(k, ()) if tk[2] != eng)
        for k in writes:
            t = self.last_write.get(k)
            if t is not None:
                deps.append(t)
            deps.extend(self.readers.get(k, ()))
        return deps

    def _record(self, reads, writes, token):
        for k in writes:
            self.last_write[k] = token
            self.readers[k] = []
        for k in reads:
            self.readers.setdefault(k, []).append(token)

    def _filter(self, eng, deps):
        out = []
        for (sem, val, src) in deps:
            if eng == "pe" and src == "pe":
                continue
            key = (eng, self._sid(sem))
            if self.waited.get(key, 0) >= val:
                continue
            self.waited[key] = val
            out.append((sem, val))
        return out

    def op(self, eng, meth, args, kw, reads=(), writes=()):
        fn = (meth, args, kw)
        deps = self._deps(reads, writes, eng)
        waits = self._filter(eng, deps)
        self.count[eng] += 1
        sem = self.eng_sem[eng]
        token = (sem, self.count[eng], eng)
        self.ops[eng].append((waits, fn, sem, 1))
        self._record(reads, writes, token)
        self.n_ops += 1
        return token

    def dma(self, q, out_ap, in_ap, reads=(), writes=()):
        k = self.dma_rr[q]
        self.dma_rr[q] = (k + 1) % len(self.dma_sems[q])
        sem = self.dma_sems[q][k]
        prev = self.dma_val[q][k]
        deps = self._deps(reads, writes)
        if prev > 0:
            deps.append((sem, prev, "dma"))
        waits = self._filter(q, deps)
        val = prev + 16
        self.dma_val[q][k] = val
        token = (sem, val, "dma")
        self.ops[q].append((waits, ("dma_start", (), {"out": out_ap, "in_": in_ap}), sem, 16))
        self._record(reads, writes, token)
        self.n_ops += 1
        return token

    def barrier(self, skip_q=()):
        toks = []
        for e in self.ENGS:
            if self.count[e] > 0:
                toks.append((self.eng_sem[e], self.count[e], e))
        for q, sems in self.dma_sems.items():
            if q in skip_q:
                continue
            for s, v in zip(sems, self.dma_val[q]):
                if v > 0:
                    toks.append((s, v, "dma"))
        for e in self.ENGS:
            waits = []
            for (sem, val, src) in toks:
                if src == e:
                    continue
                key = (e, self._sid(sem))
                if self.waited.get(key, 0) >= val:
                    continue
                self.waited[key] = val
                waits.append((sem, val))
            if waits:
                self.ops[e].append((waits, None, None, 0))
        self.last_write = {}
        self.readers = {}

    def emit(self, block):
        nc = self.nc
        engmap = {"pe": block.tensor, "act": block.scalar, "dve": block.vector, "pool": block.gpsimd, "sp": block.sync}
        for ename in self.ENGS:
            ops = self.ops[ename]

            def body(e, ops=ops):
                for (waits, fn, sem, inc) in ops:
                    for (s, v) in waits:
                        e.wait_ge(s, v)
                    if fn is not None:
                        meth, args, kw = fn
                        getattr(e, meth)(*args, **kw).then_inc(sem, inc)
            engmap[ename](body)


def _bf(a):
    return np.ascontiguousarray(a.astype(np.float32)).astype(ml_dtypes.bfloat16)


def make_consts():
    c = {}
    f32 = np.float32
    c["ident_f"] = np.eye(128, dtype=f32)
    c["ident_b"] = _bf(np.eye(128))
    c["ones_b"] = _bf(np.ones((128, 128)))
    P = np.zeros((128, 128), f32)
    for fp in range(128):
        if fp % 32 < 16:
            P[fp + 16, fp] = -1.0
        else:
            P[fp - 16, fp] = 1.0
    c["prot_f"] = P
    hm = np.zeros((128, 4), f32)
    for p in range(128):
        hm[p, p // 32] = 1.0
    c["hm"] = hm
    bd = np.zeros((128, 256), f32)
    for p in range(128):
        h = p // 32
        bd[p, h * 64:(h + 1) * 64] = 1.0
    c["bd"] = bd
    j = np.arange(128)[:, None].astype(f32)
    i = np.arange(128)[None, :].astype(f32)
    c["posd"] = np.maximum(i - j, 0.0).astype(f32)
    c["negd"] = np.maximum(j - i, 0.0).astype(f32)
    c["ge"] = (i >= j).astype(f32)
    c["le"] = (i <= j).astype(f32)
    c["ip1"] = np.broadcast_to(i + 1.0, (128, 128)).astype(f32).copy()
    c["cmi"] = np.broadcast_to(128.0 - i, (128, 128)).astype(f32).copy()
    cols = np.zeros((128, 4), f32)
    cols[:, 0] = 127.0 - np.arange(128)
    cols[:, 1] = np.arange(128)
    cols[:, 2] = 1.0
    cols[0, 2] = 0.0
    cols[:, 3] = 1.0
    c["cols"] = cols
    c["mprev"] = _bf((j >= i).astype(f32))
    c["mnext"] = _bf((j <= i).astype(f32))
    pos = np.arange(L, dtype=np.float64)
    inv_ret = 1.0 / (10000.0 ** np.linspace(0.0, 1.0, 16))
    ang = pos[:, None] * inv_ret[None, :].astype(np.float32).astype(np.float64)
    ang = np.concatenate([ang, ang], -1)
    ang = (pos.astype(np.float32)[:, None] * inv_ret.astype(np.float32)[None, :]).astype(np.float32)
    ang = np.concatenate([ang, ang], -1)
    cr, sr = np.cos(ang).astype(f32), np.sin(ang).astype(f32)
    ks = 32.0 ** -0.5
    tab = np.zeros((4, 128, TOK), f32)
    for h in range(4):
        tab[0, h * 32:(h + 1) * 32, :L] = cr.T
        tab[1, h * 32:(h + 1) * 32, :L] = sr.T
    tab[0, :, L:] = 1.0
    tab[2] = tab[0] * ks
    tab[3] = tab[1] * ks
    c["rtab"] = tab
    half = 32
    inv_ax = (10000.0 ** (-np.arange(0, half, 2, dtype=np.float32) / half)).astype(f32)
    row = (np.arange(L) // 64).astype(f32)
    col = (np.arange(L) % 64).astype(f32)
    ar = np.concatenate([row[:, None] * inv_ax[None, :]] * 2, -1)
    ac = np.concatenate([col[:, None] * inv_ax[None, :]] * 2, -1)
    atab = np.zeros((2, 128, TOK), f32)
    for hh in range(2):
        atab[0, hh * 64:hh * 64 + 32, :L] = np.cos(ar).T
        atab[0, hh * 64 + 32:hh * 64 + 64, :L] = np.cos(ac).T
        atab[1, hh * 64:hh * 64 + 32, :L] = np.sin(ar).T
        atab[1, hh * 64 + 32:hh * 64 + 64, :L] = np.sin(ac).T
    atab[0, :, L:] = 1.0
    c["atab"] = atab
    for name, Lh in (("l", L), ("c", LC)):
        t = np.linspace(0.0, 1.0, Lh, dtype=f32)[:, None]
        w = (2.0 * math.pi * np.arange(Lh, dtype=f32) / Lh).astype(f32)
        bands = np.linspace(1e-4, 15, 16, dtype=f32)
        angz = w[:, None] * bands[None, :]
        z = np.concatenate([t, np.cos(angz), -np.sin(angz)], -1).astype(f32)
        c["zembT_" + name] = np.ascontiguousarray(z.T)
        max_decay = math.log(1e-2) / 0.3
        min_decay = math.log(1e-2) / 1.5
        deltas = np.linspace(min_decay, max_decay, HY_CH, dtype=f32)
        decay = np.exp(-t * np.abs(deltas)[None, :]).astype(f32)
        ntc = Lh // 128
        c["decay_" + name] = np.ascontiguousarray(decay.reshape(ntc, 128, HY_CH).transpose(1, 0, 2))
        k = np.arange(Lh, dtype=np.float64)[:, None]
        tt = np.arange(Lh, dtype=np.float64)[None, :]
        ang2 = (math.pi / Lh) * (k + 0.5) * tt
        C = np.cos(ang2)
        S = np.sin(ang2)
        nk = Lh // 128
        fw = np.zeros((nk, 2, 128, ntc, 128), np.float32)
        for kc in range(nk):
            Ct = C[kc * 128:(kc + 1) * 128, :].T.reshape(ntc, 128, 128).transpose(1, 0, 2)
            St = S[kc * 128:(kc + 1) * 128, :].T.reshape(ntc, 128, 128).transpose(1, 0, 2)
            fw[kc, 0] = Ct
            fw[kc, 1] = St
        c["dftf_" + name] = _bf(fw)
        tb = min(512, Lh)
        ntb = Lh // tb
        iv = np.zeros((ntb, 128, nk, 2, tb), np.float32)
        for b in range(ntb):
            Cb = (C[:, b * tb:(b + 1) * tb] / Lh).reshape(nk, 128, tb).transpose(1, 0, 2)
            Sb = (-S[:, b * tb:(b + 1) * tb] / Lh).reshape(nk, 128, tb).transpose(1, 0, 2)
            iv[b, :, :, 0, :] = Cb
            iv[b, :, :, 1, :] = Sb
        c["dfti_" + name] = _bf(iv)
    return c


_CONSTS = None


def get_consts():
    global _CONSTS
    if _CONSTS is None:
        _CONSTS = make_consts()
    return _CONSTS


W_NAMES = [
    ("w_ada", [DEPTH, D, 6 * D]), ("b_ada", [DEPTH, 6 * D]), ("g_pre_mix", [DEPTH, D]), ("g_post_mix", [DEPTH, D]),
    ("g_pre_mlp", [DEPTH, D]), ("g_post_mlp", [DEPTH, D]), ("w_in", [DEPTH, D, D_IN]),
    ("ret_decay_fwd", [DEPTH, 4]), ("ret_decay_bwd", [DEPTH, 4]), ("hy_short_w", [DEPTH, 3, 768]),
    ("hy_short_b", [DEPTH, 768]), ("hy_f_w1", [DEPTH, 33, 64]), ("hy_f_b1", [DEPTH, 64]),
    ("hy_f_w2", [DEPTH, 2, 64, 64]), ("hy_f_b2", [DEPTH, 2, 64]), ("hy_f_w3", [DEPTH, 64, 512]),
    ("hy_f_freq", [DEPTH, 64]), ("hy_bias", [DEPTH, 256]), ("attn_sink", [DEPTH, 8]), ("g_ret", [DEPTH, 256]),
    ("g_hy", [DEPTH, 256]), ("g_att", [DEPTH, 512]), ("w_out", [DEPTH, D, D]), ("w_ff1", [DEPTH, D, D_FF]),
    ("w_ff2", [DEPTH, D_FF, D]),
]

ARENA_BYTES = 56320
PA1, PS1, PG1, PA2, PS2, PG2 = range(6)


class Builder:
    def __init__(self, NB=4, NL=4, dbg=None):
        self.NB, self.NL, self.dbg = NB, NL, dbg
        self.NBX = NB + 1
        self.nc = bass.Bass("TRN2", target_bir_lowering=False)
        self.C = get_consts()

    def av(self, off, shape, dt):
        esz = 4 if dt == F32 else 2
        nel = int(np.prod(shape[1:]))
        assert off % 4 == 0 and off + nel * esz <= ARENA_BYTES, (off, shape)
        a = self.ARENA[0:shape[0], off // 2: off // 2 + nel * esz // 2]
        if dt == F32:
            a = a.bitcast(F32)
        if len(shape) == 3:
            a = a.rearrange("p (a b) -> p a b", a=shape[1])
        elif len(shape) == 4:
            a = a.rearrange("p (a b c) -> p a b c", a=shape[1], b=shape[2])
        return a

    def bv(self, off_el, shape):
        nel = int(np.prod(shape[1:]))
        a = self.BIG[:, off_el: off_el + nel]
        if len(shape) == 3:
            a = a.rearrange("p (a b) -> p a b", a=shape[1])
        return a

    def bigv(self, off, shape, dt, base_el=10 * TOK):
        esz = 4 if dt == F32 else 2
        nel = int(np.prod(shape[1:]))
        e0 = base_el + off // 2
        assert off % 4 == 0 and e0 + nel * esz // 2 <= 36864, (off, shape)
        a = self.BIG[0:shape[0], e0: e0 + nel * esz // 2]
        if dt == F32:
            a = a.bitcast(F32)
        if len(shape) == 3:
            a = a.rearrange("p (a b) -> p a b", a=shape[1])
        return a

    def psb(self):
        i = self.ps_rr
        self.ps_rr = (i + 1) % 8
        return i

    def O(self, eng, meth, *args, r=(), w=(), **kw):
        return self.p.op(eng, meth, args, kw, list(r), list(w))

    def DMA(self, q, out_ap, in_ap, r=(), w=()):
        return self.p.dma(q, out_ap, in_ap, list(r), list(w))

    def build(self):
        from contextlib import ExitStack
        nc = self.nc
        NB, NBX = self.NB, self.NBX
        self.d = {}
        self.d["x"] = nc.dram_tensor("x", [NB, L, D], F32, kind="ExternalInput").ap()
        self.d["ctx"] = nc.dram_tensor("ctx", [NB, LC, D], F32, kind="ExternalInput").ap()
        self.d["c"] = nc.dram_tensor("c", [NB, D], F32, kind="ExternalInput").ap()
        self.d["c_ctx"] = nc.dram_tensor("c_ctx", [1, D], F32, kind="ExternalInput").ap()
        for name, shp in W_NAMES:
            self.d[name] = nc.dram_tensor(name, shp, F32, kind="ExternalInput").ap()
        self.k = {}
        for name, arr in self.C.items():
            dt = F32 if arr.dtype == np.float32 else BF16
            self.k[name] = nc.dram_tensor("k_" + name, list(arr.shape), dt, kind="ExternalInput").ap()
        self.out_d = nc.dram_tensor("out", [NB, L, D], F32, kind="ExternalOutput").ap()
        self.hf = {"l": nc.dram_tensor("hf_l", [DEPTH, 128, L // 128, 2, 256], BF16, kind="Internal").ap(),
                   "c": nc.dram_tensor("hf_c", [DEPTH, 128, LC // 128, 2, 256], BF16, kind="Internal").ap()}
        self.wb = {}
        for name, shp in (("w_in", [DEPTH, D, D_IN]), ("w_out", [DEPTH, D, D]), ("w_ff1", [DEPTH, D, D_FF]), ("w_ff2", [DEPTH, D_FF, D])):
            self.wb[name] = nc.dram_tensor("wb_" + name, shp, BF16, kind="Internal").ap()
        if self.dbg:
            self.dbg_x = nc.dram_tensor("dbg_x", [128, 8 * TOK], F32, kind="ExternalOutput").ap()
            self.dbg_big = nc.dram_tensor("dbg_big", [128, 36864], BF16, kind="ExternalOutput").ap()
        with ExitStack() as es:
            def sb(name, shape, dt):
                return es.enter_context(nc.sbuf_tensor(name, shape, dt))
            self.X = sb("X", [128, 8, TOK], F32)
            self.BIG = sb("BIG", [128, 36864], BF16)
            self.ARENA = sb("ARENA", [128, ARENA_BYTES // 2], BF16)
            self.ident_f = sb("ident_f", [128, 128], F32)
            self.ident_b = sb("ident_b", [128, 128], BF16)
            self.ones_b = sb("ones_b", [128, 128], BF16)
            self.prot_f = sb("prot_f", [128, 128], F32)
            self.hm = sb("hm", [128, 4], F32)
            self.bd = sb("bd", [128, 256], F32)
            self.cols = sb("cols", [128, 4], F32)
            self.mprev = sb("mprev", [128, 128], BF16)
            self.mnext = sb("mnext", [128, 128], BF16)
            self.ones_row = sb("ones_row", [1, 128], F32)
            self.COLS = sb("COLS", [128, DEPTH, 66], F32)
            self.PAR = sb("PAR", [128, DEPTH, 6, 8, NBX], F32)
            self.LOGG = sb("LOGG", [128, 32], F32)
            self.SINKE = sb("SINKE", [128, 32], F32)
            self.HC = sb("HC", [64, DEPTH, 8], F32)
            self.PS = [es.enter_context(nc.psum_tensor("ps%d" % i, [128, 512], F32)) for i in range(8)]
            self.ps_rr = 0
            eng_sems = {e: es.enter_context(nc.semaphore("s_" + e)) for e in Prog.ENGS}
            dma_sems = {q: [es.enter_context(nc.semaphore("d_%s%d" % (q, i))) for i in range(8)] for q in ("sp", "pool")}
            self.p = Prog(nc, eng_sems, dma_sems)
            self.HT = self.bv(0, [128, 8, TOK])
            self.MIXT = self.bv(8 * TOK, [128, 8, TOK])
            self.program()
            with nc.Block() as block:
                self.p.emit(block)
        return nc

    def program(self):
        p = self.p
        self.prologue()
        for b in range(self.NB):
            self.load_x(b)
            for l in range(self.NL):
                self.layer(b, l)
            if self.dbg:
                self.dump_dbg()
            self.store_out(b)
        p.barrier()

    def prologue(self):
        p, O, DMA = self.p, self.O, self.DMA
        NB, NBX = self.NB, self.NBX
        for name in ("ident_f", "ident_b", "ones_b", "prot_f", "hm", "bd", "cols", "mprev", "mnext"):
            t = getattr(self, name)
            DMA("sp", t[:], self.k[name][:, :], w=[(name,)])
        DMA("sp", self.ones_row[0:1, :], self.k["ip1"][0:1, :], w=[("ones_row_raw",)])
        O("dve", "tensor_scalar", self.ones_row[0:1, :], self.ones_row[0:1, :], 0.0, 1.0, ALU.mult, ALU.add,
          r=[("ones_row_raw",)], w=[("ones_row",)])
        for l in range(self.NL):
            for name, rows in (("w_in", D), ("w_out", D), ("w_ff1", D), ("w_ff2", D_FF)):
                for r0 in range(0, rows, 256):
                    DMA("pool", self.wb[name][l, r0:r0 + 256, :], self.d[name][l, r0:r0 + 256, :], w=[("wb", name, l, r0)])
        CS = self.av(0, [NBX, 1024], F32)
        CSb = self.av(4096, [NBX, 1024], F32)
        SCT = self.av(10752, [128, 8, NBX], F32)
        DMA("sp", CS[0:NB, :], self.d["c"][:, :], w=[("CS", 0)])
        DMA("sp", CS[NB:NBX, :], self.d["c_ctx"][:, :], w=[("CS", 1)])
        O("act", "activation", CSb[:, :], CS[:, :], AF.Silu, r=[("CS", 0), ("CS", 1)], w=[("CSb",)])
        pb = self.psb()
        psf = self.PS[pb][:, 0:64].rearrange("p (k c) -> p k c", k=8)
        for kc in range(8):
            O("pe", "transpose", psf[:, kc, 0:NBX], CSb[0:NBX, kc * 128:(kc + 1) * 128], self.ident_f[0:NBX, 0:NBX],
              r=[("CSb",), ("ident_f",)], w=[("ps", pb)])
        O("dve", "tensor_copy", SCT[:, :, :], psf[:, :, 0:NBX], r=[("ps", pb)], w=[("SCT",)])
        STG = self.av(8192, [128, 128], F32)
        BST = self.av(8704, [48, 128], F32)
        MODT = self.av(9216, [128, 48, NBX], F32)
        BADA = self.av(9216 + 48 * NBX * 4, [128, 48], F32)
        HST = self.av(10496, [4, 64], F32)
        slab_off = 12288
        rowmap = [("g_pre_mix", 0, 8), ("g_post_mix", 8, 8), ("g_pre_mlp", 16, 8), ("g_post_mlp", 24, 8),
                  ("g_ret", 32, 2), ("g_hy", 34, 2), ("g_att", 36, 4), ("hy_short_b", 58, 6), ("hy_bias", 64, 2)]
        for l in range(self.NL):
            for name, r0, nr in rowmap:
                DMA("sp", STG[r0:r0 + nr, :], self.d[name][l].rearrange("(r c) -> r c", c=128), w=[("STG", name)])
            DMA("sp", STG[40:58, :], self.d["hy_short_w"][l].rearrange("k (r c) -> (k r) c", c=128), w=[("STG", "sw")])
            pb = self.psb()
            O("pe", "transpose", self.PS[pb][:, 0:66], STG[0:66, :], self.ident_f[0:66, 0:66],
              r=[("STG", n) for n, _, _ in rowmap] + [("STG", "sw"), ("ident_f",)], w=[("ps", pb)])
            O("dve", "tensor_copy", self.COLS[:, l, :], self.PS[pb][:, 0:66], r=[("ps", pb)], w=[("COLS", l)])
            DMA("sp", BST[0:48, :], self.d["b_ada"][l].rearrange("(r c) -> r c", c=128), w=[("BST",)])
            pb = self.psb()
            O("pe", "transpose", self.PS[pb][:, 0:48], BST[0:48, :], self.ident_f[0:48, 0:48], r=[("BST",), ("ident_f",)], w=[("ps", pb)])
            O("dve", "tensor_copy", BADA[:, :], self.PS[pb][:, 0:48], r=[("ps", pb)], w=[("BADA",)])
            for s in range(12):
                slab = self.av(slab_off + (s % 2) * 16384, [128, 8, 512], F32)
                DMA("sp", slab[:, :, :], self.d["w_ada"][l, :, s * 512:(s + 1) * 512].rearrange("(kc p) n -> p kc n", p=128),
                    w=[("adaslab", s % 2)])
                for oc in range(4):
                    pb = self.psb()
                    for kc in range(8):
                        O("pe", "matmul", self.PS[pb][:, 0:NBX], slab[:, kc, oc * 128:(oc + 1) * 128], SCT[:, kc, :],
                          start=(kc == 0), stop=(kc == 7), r=[("adaslab", s % 2), ("SCT",)], w=[("ps", pb)])
                    og = s * 4 + oc
                    O("dve", "tensor_scalar", MODT[:, og, :], self.PS[pb][:, 0:NBX], BADA[:, og:og + 1], None, ALU.add,
                      r=[("ps", pb), ("BADA",)], w=[("MODT", og)])
            allmod = [("MODT", i) for i in range(48)]

            def colb(r0):
                return self.COLS[:, l, r0:r0 + 8].unsqueeze(2).to_broadcast([128, 8, NBX])
            PARl = self.PAR[:, l]
            O("dve", "scalar_tensor_tensor", PARl[:, PA1], MODT[:, 8:16, :], 1.0, colb(0), ALU.add, ALU.mult,
              r=allmod + [("COLS", l)], w=[("PAR", l, PA1)])
            O("dve", "tensor_copy", PARl[:, PS1], MODT[:, 0:8, :], r=allmod, w=[("PAR", l, PS1)])
            O("dve", "tensor_tensor", PARl[:, PG1], MODT[:, 16:24, :], colb(8), ALU.mult, r=allmod + [("COLS", l)], w=[("PAR", l, PG1)])
            O("dve", "scalar_tensor_tensor", PARl[:, PA2], MODT[:, 32:40, :], 1.0, colb(16), ALU.add, ALU.mult,
              r=allmod + [("COLS", l)], w=[("PAR", l, PA2)])
            O("dve", "tensor_copy", PARl[:, PS2], MODT[:, 24:32, :], r=allmod, w=[("PAR", l, PS2)])
            O("dve", "tensor_tensor", PARl[:, PG2], MODT[:, 40:48, :], colb(24), ALU.mult, r=allmod + [("COLS", l)], w=[("PAR", l, PG2)])
            DMA("sp", HST[0:1, :], self.d["hy_f_b1"][l:l + 1, :], w=[("HST", 0)])
            DMA("sp", HST[1:3, :], self.d["hy_f_b2"][l], w=[("HST", 1)])
            DMA("sp", HST[3:4, :], self.d["hy_f_freq"][l:l + 1, :], w=[("HST", 2)])
            pb = self.psb()
            O("pe", "transpose", self.PS[pb][0:64, 0:4], HST[0:4, :], self.ident_f[0:4, 0:4],
              r=[("HST", 0), ("HST", 1), ("HST", 2), ("ident_f",)], w=[("ps", pb)])
            O("dve", "tensor_copy", self.HC[:, l, 0:4], self.PS[pb][0:64, 0:4], r=[("ps", pb)], w=[("HC", l, 0)])
            O("dve", "tensor_scalar", self.HC[:, l, 4:7], self.HC[:, l, 0:3], self.HC[:, l, 3:4], None, ALU.mult,
              r=[("HC", l, 0)], w=[("HC", l, 1)])
            p.barrier(skip_q=("pool",))
        for l in range(self.NL):
            for nm, Lh in (("l", L), ("c", LC)):
                ntc = Lh // 128
                RT = self.av(0, [128, ntc, 3, 256], BF16)
                self.hyena_filter(l, nm, Lh, RT)
                DMA("sp", self.hf[nm][l, :, :, 0, :], RT[:, :, 0, :], r=[("RT", t, 0) for t in range(ntc)], w=[("hf", nm, l, 0)])
                DMA("sp", self.hf[nm][l, :, :, 1, :], RT[:, :, 2, :], r=[("RT", t, 2) for t in range(ntc)], w=[("hf", nm, l, 1)])
                p.barrier(skip_q=("pool",))
        DR = self.av(0, [1, 64], F32)
        DMA("sp", DR[0:1, 0:16], self.d["ret_decay_fwd"].rearrange("(o a) b -> o (a b)", o=1), w=[("DR", 0)])
        DMA("sp", DR[0:1, 16:32], self.d["ret_decay_bwd"].rearrange("(o a) b -> o (a b)", o=1), w=[("DR", 1)])
        DMA("sp", DR[0:1, 32:64], self.d["attn_sink"].rearrange("(o a) b -> o (a b)", o=1), w=[("DR", 2)])
        pb = self.psb()
        O("pe", "matmul", self.PS[pb][:, 0:64], self.ones_row[0:1, :], DR[0:1, :], start=True, stop=True,
          r=[("DR", 0), ("DR", 1), ("DR", 2), ("ones_row",)], w=[("ps", pb)])
        ET = self.av(256, [128, 32], F32)
        O("act", "activation", ET[:, :], self.PS[pb][:, 0:32], AF.Exp, r=[("ps", pb)], w=[("ET",)])
        O("act", "activation", self.LOGG[:, :], ET[:, :], AF.Ln, bias=1.0, scale=-1.0, r=[("ET",)], w=[("LOGG",)])
        O("act", "activation", self.SINKE[:, :], self.PS[pb][:, 32:64], AF.Exp, bias=-ATT_M, scale=1.0, r=[("ps", pb)], w=[("SINKE",)])
        p.barrier()

    def load_x(self, b):
        p, O, DMA = self.p, self.O, self.DMA
        for t in range(NT):
            st = self.av((t % 2) * 4096, [128, 1024], F32)
            src = self.d["x"][b, t * 128:(t + 1) * 128, :] if t < 16 else self.d["ctx"][b, (t - 16) * 128:(t - 15) * 128, :]
            DMA("sp", st[:, :], src, w=[("xst", t % 2)])
            for half in range(2):
                pb = self.psb()
                for q in range(4):
                    kc = half * 4 + q
                    O("pe", "transpose", self.PS[pb][:, q * 128:(q + 1) * 128], st[:, kc * 128:(kc + 1) * 128], self.ident_f[:, :],
                      r=[("xst", t % 2), ("ident_f",)], w=[("ps", pb)])
                dst = self.X[:, half * 4:(half + 1) * 4, t * 128:(t + 1) * 128]
                srcp = self.PS[pb][:, :].rearrange("p (a b) -> p a b", a=4)
                wk = [("X", kc, t) for kc in range(half * 4, half * 4 + 4)]
                if half == 0:
                    O("act", "copy", dst, srcp, r=[("ps", pb)], w=wk)
                else:
                    O("dve", "tensor_copy", dst, srcp, r=[("ps", pb)], w=wk)
        p.barrier()

    def store_out(self, b):
        p, O, DMA = self.p, self.O, self.DMA
        p.barrier()
        for t in range(16):
            st = self.av((t % 2) * 4096, [128, 1024], F32)
            for half in range(2):
                pb = self.psb()
                for q in range(4):
                    kc = half * 4 + q
                    O("pe", "transpose", self.PS[pb][:, q * 128:(q + 1) * 128], self.X[:, kc, t * 128:(t + 1) * 128], self.ident_f[:, :],
                      r=[("X", kc, t), ("ident_f",)], w=[("ps", pb)])
                dst = st[:, half * 512:(half + 1) * 512]
                if half == 0:
                    O("act", "copy", dst, self.PS[pb][:, :], r=[("ps", pb)], w=[("ost", t % 2, half)])
                else:
                    O("dve", "tensor_copy", dst, self.PS[pb][:, :], r=[("ps", pb)], w=[("ost", t % 2, half)])
            DMA("sp", self.out_d[b, t * 128:(t + 1) * 128, :], st[:, :], r=[("ost", t % 2, 0), ("ost", t % 2, 1)], w=[("outd", b, t)])
        p.barrier()

    def dump_dbg(self):
        p = self.p
        p.barrier()
        self.DMA("sp", self.dbg_x[:, :], self.X[:, :, :].rearrange("p a b -> p (a b)"), w=[("dbgx",)])
        self.DMA("sp", self.dbg_big[:, :], self.BIG[:, :], w=[("dbgb",)])
        p.barrier()

    def prenorm(self, b, l, kA, kS, groups, dst, dst_t0, aoff, dkey="HT"):
        O = self.O
        SQ = self.av(aoff, [128, 8, 512], BF16)
        for gi, (t0, n) in enumerate(groups):
            RS = self.av(aoff + 8192 + (gi % 2) * 2048, [128, 512], F32)
            bcol = b if t0 < L else self.NB
            tl = tiles_of(t0, n)
            for kc in range(8):
                xin = self.X[:, kc, t0:t0 + n]
                rk = [("X", kc, t) for t in tl]
                if kc % 2 == 0:
                    O("act", "activation", SQ[:, kc, 0:n], xin, AF.Square, r=rk, w=[("SQ", kc)])
                else:
                    O("pool", "tensor_tensor", SQ[:, kc, 0:n], xin, xin, ALU.mult, r=rk, w=[("SQ", kc)])
            pb = self.psb()
            for kc in range(8):
                O("pe", "matmul", self.PS[pb][:, 0:n], self.ones_b[:, :], SQ[:, kc, 0:n], start=(kc == 0), stop=(kc == 7),
                  r=[("SQ", kc), ("ones_b",)], w=[("ps", pb)])
            O("act", "activation", RS[:, 0:n], self.PS[pb][:, 0:n], AF.Sqrt, bias=EPS, scale=1.0 / D, r=[("ps", pb)], w=[("RS", gi % 2)])
            O("dve", "reciprocal", RS[:, 0:n], RS[:, 0:n], r=[("RS", gi % 2)], w=[("RS", gi % 2)])
            for kc in range(8):
                TMP = self.av(aoff + 12288 + (kc % 2) * 2048, [128, 512], F32)
                a_col = self.PAR[:, l, kA, kc, bcol:bcol + 1]
                s_col = self.PAR[:, l, kS, kc, bcol:bcol + 1]
                xin = self.X[:, kc, t0:t0 + n]
                O("dve", "scalar_tensor_tensor", TMP[:, 0:n], xin, a_col, RS[:, 0:n], ALU.mult, ALU.mult,
                  r=[("X", kc, t) for t in tl] + [("RS", gi % 2), ("PAR", l, kA)], w=[("PNT", kc % 2)])
                o = dst[:, kc, t0 - dst_t0: t0 - dst_t0 + n]
                O("act", "activation", o, TMP[:, 0:n], AF.Identity, bias=s_col, scale=1.0,
                  r=[("PNT", kc % 2), ("PAR", l, kS)], w=[(dkey, kc, t) for t in tl])

    def layer(self, b, l):
        p = self.p
        p.barrier()
        self.prenorm(b, l, PA1, PS1, GROUPS, self.HT, 0, 0)
        if self.dbg == "prenorm":
            return
        if self.dbg in (None, "mix", "wout", "full", "ret"):
            self.retention(b, l)
        else:
            p.barrier()
        p.barrier()
        if self.dbg in (None, "mix", "wout", "full", "hy"):
            self.hyena(b, l)
        p.barrier()
        if self.dbg in (None, "mix", "wout", "full", "att"):
            self.attention(b, l)
        p.barrier()
        if self.dbg in ("mix", "ret", "hy", "att"):
            return
        self.wout(b, l)
        p.barrier()
        if self.dbg == "wout":
            return
        self.ffn(b, l)
        p.barrier()

    def load_win(self, l, slab, c0, ncols, key):
        self.DMA("sp", slab, self.wb["w_in"][l, :, c0:c0 + ncols].rearrange("(kc p) n -> p kc n", p=128), w=[key])

    def proj_fm(self, lhs_slab, col_off, t0, n, key):
        pb = self.psb()
        tl = tiles_of(t0, n)
        for kc in range(8):
            self.O("pe", "matmul", self.PS[pb][:, 0:n], lhs_slab[:, kc, col_off:col_off + 128], self.HT[:, kc, t0:t0 + n],
                   start=(kc == 0), stop=(kc == 7), r=[key] + [("HT", kc, t) for t in tl], w=[("ps", pb)])
        return pb

    def proj_tm(self, rhs_slab, col_off, ncols, tile, key):
        pb = self.psb()
        for kc in range(8):
            self.O("pe", "matmul", self.PS[pb][:, 0:ncols], self.HT[:, kc, tile * 128:(tile + 1) * 128], rhs_slab[:, kc, col_off:col_off + ncols],
                   start=(kc == 0), stop=(kc == 7), r=[key, ("HT", kc, tile)], w=[("ps", pb)])
        return pb

    def rot_bufs(self, base_chunk, ntab):
        base = (8 + base_chunk) * TOK
        def fv(i):
            assert base + (i + 1) * 1024 <= 36864
            return self.BIG[:, base + i * 1024: base + (i + 1) * 1024].bitcast(F32)
        return {"QF": [fv(0), fv(1)], "T1": [fv(2), fv(3)], "TAB": [[fv(4 + s_ * ntab + k) for k in range(ntab)] for s_ in range(2)], "i": 0}

    def rotary(self, pb, n, dst, dkeys, RB, COS, SIN, ckeys):
        O = self.O
        si = RB["i"] % 2
        RB["i"] += 1
        QF, T1 = RB["QF"][si], RB["T1"][si]
        kq, kt = ("QF", si), ("T1", si)
        O("act", "copy", QF[:, 0:n], self.PS[pb][:, 0:n], r=[("ps", pb)], w=[kq])
        pb2 = self.psb()
        O("pe", "matmul", self.PS[pb2][:, 0:n], self.prot_f[:, :], QF[:, 0:n], start=True, stop=True, r=[kq, ("prot_f",)], w=[("ps", pb2)])
        O("dve", "tensor_tensor", T1[:, 0:n], QF[:, 0:n], COS[:, 0:n], ALU.mult, r=[kq] + ckeys, w=[kt])
        O("dve", "tensor_tensor", QF[:, 0:n], self.PS[pb2][:, 0:n], SIN[:, 0:n], ALU.mult, r=[("ps", pb2)] + ckeys, w=[kq])
        if isinstance(dst, list):
            for (p0, p1, d) in dst:
                O("dve", "tensor_tensor", d, T1[p0:p1, 0:n], QF[p0:p1, 0:n], ALU.add, r=[kt, kq], w=dkeys)
        else:
            O("dve", "tensor_tensor", dst, T1[:, 0:n], QF[:, 0:n], ALU.add, r=[kt, kq], w=dkeys)

    def retention(self, b, l):
        p, O, DMA = self.p, self.O, self.DMA
        RQT = self.av(0, [128, TOK], BF16)
        RKT = self.av(4608, [128, TOK], BF16)
        RV = self.av(9216, [128, NT, 256], BF16)
        RG = self.av(18432, [128, NT, 256], BF16)
        SF = self.av(27648, [128, NT, 256], BF16)
        SB = self.av(36864, [128, NT, 256], BF16)
        DMT = self.av(46080, [128, 4, 128], F32)
        QDF = self.av(48128, [128, 128], F32)
        QDB = self.av(48640, [128, 128], F32)
        KD4 = self.av(49152, [128, 8], F32)
        GC = self.av(49184, [128, 2], F32)
        LFM = self.av(49192, [128, 2], F32)
        LT4 = self.av(49200, [128, 4], F32)
        IT = self.av(27648, [128, 6, 128], F32)
        for i, nm in enumerate(("posd", "negd", "ge", "le", "ip1", "cmi")):
            DMA("sp", IT[:, i, :], self.k[nm][:, :], w=[("IT", i)])
        TA = self.av(27648 + 3072, [128, 128], F32)
        TB = self.av(27648 + 3584, [128, 128], F32)
        lgf = lambda h: self.LOGG[:, l * 4 + h: l * 4 + h + 1]
        lgb = lambda h: self.LOGG[:, 16 + l * 4 + h: 16 + l * 4 + h + 1]
        for h in range(4):
            O("act", "activation", TA[:, :], IT[:, 0, :], AF.Exp, scale=lgf(h), r=[("IT", 0), ("LOGG",)], w=[("TA",)])
            O("dve", "tensor_tensor", DMT[:, h, :], TA[:, :], IT[:, 2, :], ALU.mult, r=[("TA",), ("IT", 2)], w=[("DMT", h)])
            O("act", "activation", TB[:, :], IT[:, 1, :], AF.Exp, scale=lgb(h), r=[("IT", 1), ("LOGG",)], w=[("TB",)])
            O("dve", "tensor_tensor", TB[:, :], TB[:, :], IT[:, 3, :], ALU.mult, r=[("TB",), ("IT", 3)], w=[("TB",)])
            O("dve", "tensor_tensor", DMT[:, h, :], DMT[:, h, :], TB[:, :], ALU.add, r=[("TB",), ("DMT", h)], w=[("DMT", h)])
        for d in range(2):
            O("dve", "tensor_tensor", LT4[:, :], self.LOGG[:, d * 16 + l * 4: d * 16 + l * 4 + 4], self.hm[:, :], ALU.mult,
              r=[("LOGG",), ("hm",)], w=[("LT4",)])
            O("dve", "reduce_sum", LFM[:, d:d + 1], LT4[:, :], AX.X, r=[("LT4",)], w=[("LFM", d)])
            O("act", "activation", GC[:, d:d + 1], LFM[:, d:d + 1], AF.Exp, scale=128.0, r=[("LFM", d)], w=[("GC", d)])
        O("act", "activation", QDF[:, :], IT[:, 4, :], AF.Exp, scale=LFM[:, 0:1], r=[("IT", 4), ("LFM", 0)], w=[("QDF",)])
        O("act", "activation", QDB[:, :], IT[:, 5, :], AF.Exp, scale=LFM[:, 1:2], r=[("IT", 5), ("LFM", 1)], w=[("QDB",)])
        O("act", "activation", KD4[:, 0:4], self.LOGG[:, l * 4: l * 4 + 4], AF.Exp, scale=self.cols[:, 0:1], r=[("LOGG",), ("cols",)], w=[("KD4", 0)])
        O("act", "activation", KD4[:, 4:8], self.LOGG[:, 16 + l * 4: 16 + l * 4 + 4], AF.Exp, scale=self.cols[:, 1:2], r=[("LOGG",), ("cols",)], w=[("KD4", 1)])
        p.barrier()
        S0 = self.av(27648, [128, 8, 256], BF16)
        S1 = self.av(27648 + 4096, [128, 8, 256], BF16)
        S2 = self.av(27648 + 8192, [128, 8, 256], BF16)
        RBF = self.rot_bufs(2, 4)
        self.load_win(l, S0, 0, 256, ("S0",))
        self.load_win(l, S1, 256, 256, ("S1",))
        self.load_win(l, S2, 512, 256, ("S2",))
        for gi, (t0, n) in enumerate(GROUPS):
            TAB = RBF["TAB"][gi % 2]
            for k in range(4):
                DMA("sp", TAB[k][:, 0:n], self.k["rtab"][k, :, t0:t0 + n], w=[("RTAB", gi % 2, k)])
            for which in range(2):
                dstT = RQT if which == 0 else RKT
                pb = self.proj_fm(S0, which * 128, t0, n, ("S0",))
                self.rotary(pb, n, dstT[:, t0:t0 + n], [("RQK", which, t) for t in tiles_of(t0, n)], RBF, TAB[2 * which], TAB[2 * which + 1],
                            [("RTAB", gi % 2, 2 * which), ("RTAB", gi % 2, 2 * which + 1)])
        for t in range(NT):
            pbv = self.proj_tm(S1, 0, 256, t, ("S1",))
            O("act", "copy", RV[:, t, :], self.PS[pbv][:, 0:256], r=[("ps", pbv)], w=[("RV", t)])
            pbg = self.proj_tm(S2, 0, 256, t, ("S2",))
            O("act", "activation", RG[:, t, :], self.PS[pbg][:, 0:256], AF.Silu, r=[("ps", pbg)], w=[("RG", t)])
        p.barrier()
        KDb = [self.bigv(0 + d * 256, [128, 4, 32], BF16) for d in range(2)]
        KVM = [self.bigv(512 + d * 1024, [128, 256], F32) for d in range(2)]
        ST = self.bigv(2560, [128, 2, 256], F32)
        KMb = [self.bigv(4608 + i * 1024, [128, 4, 128], BF16) for i in range(2)]
        PTb = [self.bigv(6656 + i * 1024, [128, 512], BF16) for i in range(2)]
        QDb = [self.bigv(8704 + i * 512, [128, 2, 128], BF16) for i in range(2)]
        Rb = [self.bigv(9728 + i * 1024, [128, 256], F32) for i in range(2)]
        RBb = [self.bigv(11776 + i * 512, [128, 256], BF16) for i in range(2)]
        SSb = [self.bigv(12800 + i * 32, [128, 8], F32) for i in range(2)]
        SS2b = [self.bigv(12864 + i * 32, [128, 8], F32) for i in range(2)]

        def kv_step(n, d, first, save_to):
            KD, R = KDb[d], KVM[d]
            pbt = self.psb()
            ktok = self.PS[pbt][:, 0:64].bitcast(BF16)
            O("pe", "transpose", ktok, RKT[:, n * 128:(n + 1) * 128], self.ident_b[:, :], r=[("RQK", 1, n), ("ident_b",)], w=[("ps", pbt)])
            kdc = KD4[:, d * 4: d * 4 + 4].unsqueeze(2).to_broadcast([128, 4, 32])
            O("dve", "tensor_tensor", KD[:, :, :], ktok.rearrange("p (h k) -> p h k", h=4), kdc, ALU.mult,
              r=[("ps", pbt), ("KD4", d)], w=[("KD", d)])
            pbk = self.psb()
            O("pe", "matmul", self.PS[pbk][:, 0:256], KD[:, :, :].rearrange("p h k -> p (h k)"), RV[:, n, :], start=True, stop=True,
              r=[("KD", d), ("RV", n)], w=[("ps", pbk)])
            if first:
                O("dve", "tensor_tensor", ST[:, d, :], self.PS[pbk][:, 0:256], self.bd[:, :], ALU.mult, r=[("ps", pbk), ("bd",)], w=[("ST", d)])
            else:
                O("dve", "tensor_tensor", R[:, :], self.PS[pbk][:, 0:256], self.bd[:, :], ALU.mult, r=[("ps", pbk), ("bd",)], w=[("KVM", d)])
                O("dve", "scalar_tensor_tensor", ST[:, d, :], ST[:, d, :], GC[:, d:d + 1], R[:, :], ALU.mult, ALU.add,
                  r=[("KVM", d), ("ST", d), ("GC", d)], w=[("ST", d)])
            dstS, idx = save_to
            O("act", "copy", dstS[:, idx, :], ST[:, d, :], r=[("ST", d)], w=[("SFB", d, idx)])

        fwd = [(16, True, (SF, 17)), (17, False, (SF, 0))] + [(n, False, (SF, n + 1)) for n in range(15)]
        bwd = [(17, True, (SB, 16)), (16, False, (SB, 15))] + [(n, False, (SB, n - 1)) for n in range(15, 0, -1)]
        for (fa, ba) in zip(fwd, bwd):
            kv_step(fa[0], 0, fa[1], fa[2])
            kv_step(ba[0], 1, ba[1], ba[2])
        for n in range(NT):
            q = n % 2
            KM, PT, QD, R, RB, SS, SS2 = KMb[q], PTb[q], QDb[q], Rb[q], RBb[q], SSb[q], SS2b[q]
            for h in range(4):
                O("act", "activation", KM[:, h, :], RKT[:, n * 128:(n + 1) * 128], AF.Identity, scale=self.hm[:, h:h + 1],
                  r=[("RQK", 1, n), ("hm",)], w=[("KM", q, h)])
            pbs = self.psb()
            for h in range(4):
                O("pe", "matmul", self.PS[pbs][:, h * 128:(h + 1) * 128], KM[:, h, :], RQT[:, n * 128:(n + 1) * 128], start=True, stop=True,
                  r=[("KM", q, h), ("RQK", 0, n)], w=[("ps", pbs)])
            O("dve", "tensor_tensor", PT[:, :], self.PS[pbs][:, :], DMT[:, :, :].rearrange("p h i -> p (h i)"), ALU.mult,
              r=[("ps", pbs)] + [("DMT", h) for h in range(4)], w=[("PT", q)])
            O("pool", "tensor_tensor", QD[:, 0, :], RQT[:, n * 128:(n + 1) * 128], QDF[:, :], ALU.mult, r=[("RQK", 0, n), ("QDF",)], w=[("QD", q, 0)])
            O("pool", "tensor_tensor", QD[:, 1, :], RQT[:, n * 128:(n + 1) * 128], QDB[:, :], ALU.mult, r=[("RQK", 0, n), ("QDB",)], w=[("QD", q, 1)])
            pbo = self.psb()
            if n < 16:
                states = [(0, SF, n), (1, SB, n)]
            elif n == 16:
                states = [(1, SB, 16)]
            else:
                states = [(0, SF, 17)]
            first = True
            for (d, Sx, idx) in states:
                O("pe", "matmul", self.PS[pbo][:, 0:256], QD[:, d, :], Sx[:, idx, :], start=first, stop=False,
                  r=[("QD", q, d), ("SFB", d, idx)], w=[("ps", pbo)])
                first = False
            for h in range(4):
                O("pe", "matmul", self.PS[pbo][:, h * 64:(h + 1) * 64], PT[:, h * 128:(h + 1) * 128], RV[:, n, h * 64:(h + 1) * 64],
                  start=False, stop=(h == 3), r=[("PT", q), ("RV", n)], w=[("ps", pbo)])
            kR, kRB = ("R", q), ("RB", q)
            ps3 = self.PS[pbo][:, 0:256].rearrange("p (h d) -> p h d", h=4)
            O("act", "activation", R[:, :], self.PS[pbo][:, 0:256], AF.Square, r=[("ps", pbo)], w=[kR])
            O("dve", "reduce_sum", SS[:, 0:4], R[:, :].rearrange("p (h d) -> p h d", h=4), AX.X, r=[kR], w=[("SS", q, 0)])
            O("act", "activation", SS2[:, 0:4], SS[:, 0:4], AF.Sqrt, bias=EPS, scale=1.0 / 64, r=[("SS", q, 0)], w=[("SS2", q, 0)])
            O("dve", "reciprocal", SS2[:, 0:4], SS2[:, 0:4], r=[("SS2", q, 0)], w=[("SS2", q, 0)])
            O("dve", "tensor_tensor", R[:, :].rearrange("p (h d) -> p h d", h=4), ps3, SS2[:, 0:4].unsqueeze(2).to_broadcast([128, 4, 64]), ALU.mult,
              r=[("ps", pbo), ("SS2", q, 0), kR], w=[kR])
            O("pool", "tensor_tensor", R[:, :], R[:, :], RG[:, n, :], ALU.mult, r=[kR, ("RG", n)], w=[kR])
            O("act", "activation", RB[:, :], R[:, :], AF.Square, r=[kR], w=[kRB])
            O("dve", "reduce_sum", SS[:, 4:5], RB[:, :], AX.X, r=[kRB], w=[("SS", q, 1)])
            O("act", "activation", SS2[:, 4:5], SS[:, 4:5], AF.Sqrt, bias=EPS, scale=1.0 / 256, r=[("SS", q, 1)], w=[("SS2", q, 1)])
            O("dve", "reciprocal", SS2[:, 4:5], SS2[:, 4:5], r=[("SS2", q, 1)], w=[("SS2", q, 1)])
            O("dve", "tensor_scalar", RB[:, :], R[:, :], SS2[:, 4:5], None, ALU.mult, r=[kR, ("SS2", q, 1), kRB], w=[kRB])
            pbt = self.psb()
            tps = self.PS[pbt][:, 0:128].bitcast(BF16).rearrange("p (c t) -> p c t", c=2)
            for c in range(2):
                O("pe", "transpose", tps[:, c, :], RB[:, c * 128:(c + 1) * 128], self.ident_b[:, :], r=[kRB, ("ident_b",)], w=[("ps", pbt)])
            for c in range(2):
                O("dve", "tensor_scalar", self.MIXT[:, c, n * 128:(n + 1) * 128], tps[:, c, :], self.COLS[:, l, 32 + c:33 + c], None, ALU.mult,
                  r=[("ps", pbt), ("COLS", l)], w=[("MIXT", c, n)])

    def hyena(self, b, l):
        p, O, DMA = self.p, self.O, self.DMA
        RL = TOK + 4
        LAT0, CTX0 = 1, L + 3
        RAW = self.av(0, [128, 3, RL], F32)
        SL = self.av(27696, [128, 8, 3, 128], BF16)
        T1 = self.av(33840, [128, TOK], F32)
        T2 = self.av(43056, [128, TOK], F32)
        Z = self.MIXT[:, 4:6, :]
        X0C = self.MIXT[:, 6:8, :]
        for j in range(3):
            for c0 in (0, L + 1, L + 2, TOK + 3):
                O("pool", "memset", RAW[:, j, c0:c0 + 1], 0.0, w=[("RAWPAD", j)])
        for c in range(2):
            for j in range(3):
                col = 768 + j * 256 + c * 128
                DMA("sp", SL[:, :, j, :], self.wb["w_in"][l, :, col:col + 128].rearrange("(kc p) n -> p kc n", p=128), w=[("HSL", j)])
            for j in range(3):
                for (t0, n) in GROUPS:
                    pb = self.psb()
                    tl = tiles_of(t0, n)
                    for kc in range(8):
                        O("pe", "matmul", self.PS[pb][:, 0:n], SL[:, kc, j, :], self.HT[:, kc, t0:t0 + n], start=(kc == 0), stop=(kc == 7),
                          r=[("HSL", j)] + [("HT", kc, t) for t in tl], w=[("ps", pb)])
                    off = (LAT0 + t0) if t0 < L else (CTX0 + t0 - L)
                    O("act", "copy", RAW[:, j, off:off + n], self.PS[pb][:, 0:n], r=[("ps", pb)], w=[("RAW", j, t0)])
            rawkeys = lambda j: [("RAW", j, t0) for (t0, _) in GROUPS] + [("RAWPAD", j)]
            for j, dstT in ((0, T1), (1, T2)):
                ch = j * 2 + c
                w0 = self.COLS[:, l, 40 + 0 * 6 + ch: 41 + 0 * 6 + ch]
                w1 = self.COLS[:, l, 40 + 1 * 6 + ch: 41 + 1 * 6 + ch]
                w2 = self.COLS[:, l, 40 + 2 * 6 + ch: 41 + 2 * 6 + ch]
                bb = self.COLS[:, l, 58 + ch: 59 + ch]
                for (s0, d0, n) in ((LAT0, 0, L), (CTX0, L, LC)):
                    O("act", "activation", dstT[:, d0:d0 + n], RAW[:, j, s0:s0 + n], AF.Identity, bias=bb, scale=w1,
                      r=rawkeys(j) + [("COLS", l)], w=[("HT1", j, d0)])
                    O("dve", "scalar_tensor_tensor", dstT[:, d0:d0 + n], RAW[:, j, s0 - 1:s0 - 1 + n], w0, dstT[:, d0:d0 + n], ALU.mult, ALU.add,
                      r=rawkeys(j) + [("HT1", j, d0)], w=[("HT1", j, d0)])
                    O("dve", "scalar_tensor_tensor", dstT[:, d0:d0 + n], RAW[:, j, s0 + 1:s0 + 1 + n], w2, dstT[:, d0:d0 + n], ALU.mult, ALU.add,
                      r=rawkeys(j) + [("HT1", j, d0)], w=[("HT1", j, d0)])
            for (d0, n) in ((0, L), (L, LC)):
                O("pool", "tensor_tensor", Z[:, c, d0:d0 + n], T1[:, d0:d0 + n], T2[:, d0:d0 + n], ALU.mult,
                  r=[("HT1", 0, d0), ("HT1", 1, d0)], w=[("Z", c, d0)])
            j = 2
            ch = 4 + c
            w0 = self.COLS[:, l, 40 + 0 * 6 + ch: 41 + 0 * 6 + ch]
            w1 = self.COLS[:, l, 40 + 1 * 6 + ch: 41 + 1 * 6 + ch]
            w2 = self.COLS[:, l, 40 + 2 * 6 + ch: 41 + 2 * 6 + ch]
            bb = self.COLS[:, l, 58 + ch: 59 + ch]
            for (s0, d0, n) in ((LAT0, 0, L), (CTX0, L, LC)):
                O("act", "activation", T1[:, d0:d0 + n], RAW[:, j, s0:s0 + n], AF.Identity, bias=bb, scale=w1,
                  r=rawkeys(j) + [("COLS", l), ("Z", c, d0)], w=[("HT1", 0, d0)])
                O("dve", "scalar_tensor_tensor", T1[:, d0:d0 + n], RAW[:, j, s0 - 1:s0 - 1 + n], w0, T1[:, d0:d0 + n], ALU.mult, ALU.add,
                  r=rawkeys(j) + [("HT1", 0, d0)], w=[("HT1", 0, d0)])
                O("dve", "scalar_tensor_tensor", X0C[:, c, d0:d0 + n], RAW[:, j, s0 + 1:s0 + 1 + n], w2, T1[:, d0:d0 + n], ALU.mult, ALU.add,
                  r=rawkeys(j) + [("HT1", 0, d0)], w=[("X0C", c, d0)])
        p.barrier()
        self.hyena_seq(b, l, "l", L, 0)
        p.barrier()
        self.hyena_seq(b, l, "c", LC, L)

    def hyena_seq(self, b, l, nm, Lh, tok0):
        p, O, DMA = self.p, self.O, self.DMA
        ntc = Lh // 128
        nk = ntc
        tb = min(512, Lh)
        ntb = Lh // tb
        Z = self.MIXT[:, 4:6, :]
        X0C = self.MIXT[:, 6:8, :]
        RT = self.av(0, [128, ntc, 3, 256], BF16)
        DMA("sp", RT[:, :, 0, :], self.hf[nm][l, :, :, 0, :], w=[("RT", t, 0) for t in range(ntc)])
        DMA("sp", RT[:, :, 2, :], self.hf[nm][l, :, :, 1, :], w=[("RT", t, 2) for t in range(ntc)])
        for tcg in range(ntc):
            pb = self.psb()
            tp = self.PS[pb][:, 0:128].bitcast(BF16).rearrange("p (c k) -> p c k", c=2)
            for c in range(2):
                O("pe", "transpose", tp[:, c, :], Z[:, c, tok0 + tcg * 128: tok0 + (tcg + 1) * 128], self.ident_b[:, :],
                  r=[("Z", c, tok0), ("ident_b",)], w=[("ps", pb)])
            O("dve", "tensor_copy", RT[:, tcg, 1, :], self.PS[pb][:, 0:128].bitcast(BF16), r=[("ps", pb)], w=[("RT", tcg, 1)])
        Y = self.av(24576, [128, nk, 2, 256], BF16)
        FS = [self.av(40960 + i * 4096, [128, ntc, 128], BF16) for i in range(2)]
        G = self.av(49152, [128, 2, 256], F32)
        TT = self.av(51200, [128, 2, 256], F32)
        allrt = [("RT", t, j) for t in range(ntc) for j in range(3)]
        for kc in range(nk):
            pbs = []
            for cs in range(2):
                DMA("sp", FS[cs][:, :, :], self.k["dftf_" + nm][kc, cs, :, :, :], w=[("FS", cs)])
                pb = self.psb()
                pbs.append(pb)
                for tc in range(ntc):
                    rhs = RT[:, tc, cs:cs + 2, :].rearrange("p a c -> p (a c)")
                    O("pe", "matmul", self.PS[pb][:, :], FS[cs][:, tc, :], rhs, start=(tc == 0), stop=(tc == ntc - 1),
                      r=[("FS", cs)] + (allrt if tc == 0 else []), w=[("ps", pb)])
            pc, ps_ = pbs
            O("act", "copy", G[:, 0, :], self.PS[pc][:, 0:256], r=[("ps", pc)], w=[("G", 0)])
            O("act", "copy", G[:, 1, :], self.PS[ps_][:, 256:512], r=[("ps", ps_)], w=[("G", 1)])
            Zr = self.PS[pc][:, 256:512]
            Zs = self.PS[ps_][:, 0:256]
            O("dve", "tensor_tensor", TT[:, 0, :], Zr, G[:, 0, :], ALU.mult, r=[("ps", pc), ("G", 0)], w=[("TT", 0)])
            O("dve", "tensor_tensor", TT[:, 1, :], Zs, G[:, 1, :], ALU.mult, r=[("ps", ps_), ("G", 1)], w=[("TT", 1)])
            O("pool", "tensor_tensor", Y[:, kc, 0, :], TT[:, 0, :], TT[:, 1, :], ALU.add, r=[("TT", 0), ("TT", 1)], w=[("Y", kc, 0)])
            O("dve", "tensor_tensor", TT[:, 0, :], Zr, G[:, 1, :], ALU.mult, r=[("ps", pc), ("G", 1), ("Y", kc, 0)], w=[("TT", 0)])
            O("dve", "tensor_tensor", TT[:, 1, :], Zs, G[:, 0, :], ALU.mult, r=[("ps", ps_), ("G", 0), ("Y", kc, 0)], w=[("TT", 1)])
            O("pool", "tensor_tensor", Y[:, kc, 1, :], TT[:, 0, :], TT[:, 1, :], ALU.subtract, r=[("TT", 0), ("TT", 1)], w=[("Y", kc, 1)])
        p.barrier()
        nks = max(1, nk // 4)
        kper = nk // nks
        IS = [self.av(i * 8192, [128, kper, 2, tb], BF16) for i in range(3)]
        HYO = self.av(40960, [128, 2, 512], F32)
        SQ = self.av(45056, [128, 2, 512], BF16)
        RS = self.av(47104, [128, 512], F32)
        T = self.av(49152, [128, 512], F32)
        si = 0
        ally = [("Y", kc, j) for kc in range(nk) for j in range(2)]
        for bi in range(ntb):
            t0 = tok0 + bi * tb
            pbs = [self.psb(), self.psb()]
            for ks in range(nks):
                sl = IS[si % 3]
                key = ("IS", si % 3)
                si += 1
                DMA("sp", sl[:, :, :, :], self.k["dfti_" + nm][bi, :, ks * kper:(ks + 1) * kper, :, :], w=[key])
                for c in range(2):
                    for kci in range(kper):
                        kc = ks * kper + kci
                        for ri in range(2):
                            first = (ks == 0 and kci == 0 and ri == 0)
                            last = (ks == nks - 1 and kci == kper - 1 and ri == 1)
                            O("pe", "matmul", self.PS[pbs[c]][:, 0:tb], Y[:, kc, ri, c * 128:(c + 1) * 128], sl[:, kci, ri, :], start=first, stop=last,
                              r=[key] + (ally if first else []), w=[("ps", pbs[c])])
            for c in range(2):
                O("dve", "scalar_tensor_tensor", T[:, 0:tb], Z[:, c, t0:t0 + tb], self.COLS[:, l, 64 + c:65 + c], self.PS[pbs[c]][:, 0:tb], ALU.mult, ALU.add,
                  r=[("ps", pbs[c]), ("Z", c, tok0), ("COLS", l)], w=[("T",)])
                O("dve", "tensor_tensor", HYO[:, c, 0:tb], T[:, 0:tb], X0C[:, c, t0:t0 + tb], ALU.mult, r=[("T",), ("X0C", c, tok0)], w=[("HYO", c)])
                O("act", "activation", SQ[:, c, 0:tb], HYO[:, c, 0:tb], AF.Square, r=[("HYO", c)], w=[("SQ", c)])
            pbn = self.psb()
            for c in range(2):
                O("pe", "matmul", self.PS[pbn][:, 0:tb], self.ones_b[:, :], SQ[:, c, 0:tb], start=(c == 0), stop=(c == 1),
                  r=[("SQ", c), ("ones_b",)], w=[("ps", pbn)])
            O("act", "activation", RS[:, 0:tb], self.PS[pbn][:, 0:tb], AF.Sqrt, bias=EPS, scale=1.0 / 256, r=[("ps", pbn)], w=[("RS",)])
            O("dve", "reciprocal", RS[:, 0:tb], RS[:, 0:tb], r=[("RS",)], w=[("RS",)])
            for c in range(2):
                O("dve", "scalar_tensor_tensor", self.MIXT[:, 2 + c, t0:t0 + tb], HYO[:, c, 0:tb], self.COLS[:, l, 34 + c:35 + c], RS[:, 0:tb], ALU.mult, ALU.mult,
                  r=[("HYO", c), ("RS",), ("COLS", l)], w=[("MIXT", 2 + c, t) for t in tiles_of(t0, tb)])

    def hyena_filter(self, l, nm, Lh, RT):
        p, O, DMA = self.p, self.O, self.DMA
        fo = 24576
        ZE = self.av(fo, [33, 512], F32)
        HA = self.av(fo + 2048, [64, 512], F32)
        HB = self.av(fo + 4096, [64, 512], F32)
        ARG = self.av(fo + 6144, [64, 512], F32)
        M1 = self.av(fo + 8192, [64, 512], F32)
        W1 = self.av(fo + 10240, [33, 64], F32)
        W2 = self.av(fo + 10496, [64, 2, 64], F32)
        W3 = self.av(fo + 11008, [64, 512], F32)
        DEC = self.av(fo + 13056, [128, 2, 256], F32)
        HF = self.av(fo + 15104, [128, 512], F32)
        DMA("sp", W1[:, :], self.d["hy_f_w1"][l], w=[("W1",)])
        DMA("sp", W2[:, :, :], self.d["hy_f_w2"][l].rearrange("j i o -> i j o"), w=[("W2",)])
        DMA("sp", W3[:, :], self.d["hy_f_w3"][l], w=[("W3",)])
        PI = math.pi
        for g0 in range(0, Lh, 512):
            n = min(512, Lh - g0)
            DMA("sp", ZE[:, 0:n], self.k["zembT_" + nm][:, g0:g0 + n], w=[("ZE",)])
            cur_in, cur_w, kdim = ZE, W1[:, :], 33
            hbuf = [HA, HB, HA]
            for li in range(3):
                pb = self.psb()
                O("pe", "matmul", self.PS[pb][0:64, 0:n], cur_w, cur_in[0:kdim, 0:n], start=True, stop=True,
                  r=[("ZE",), ("W1",), ("W2",), ("HH", 0), ("HH", 1)], w=[("ps", pb)])
                O("act", "activation", ARG[:, 0:n], self.PS[pb][0:64, 0:n], AF.Identity, bias=self.HC[:, l, 4 + li:5 + li], scale=self.HC[:, l, 3:4],
                  r=[("ps", pb), ("HC", l, 0), ("HC", l, 1)], w=[("ARG",)])
                for _ in range(2):
                    O("dve", "tensor_scalar", M1[:, 0:n], ARG[:, 0:n], PI, -2.0 * PI, ALU.is_gt, ALU.mult, r=[("ARG",)], w=[("M1",)])
                    O("dve", "tensor_tensor", ARG[:, 0:n], ARG[:, 0:n], M1[:, 0:n], ALU.add, r=[("ARG",), ("M1",)], w=[("ARG",)])
                    O("dve", "tensor_scalar", M1[:, 0:n], ARG[:, 0:n], -PI, 2.0 * PI, ALU.is_lt, ALU.mult, r=[("ARG",)], w=[("M1",)])
                    O("dve", "tensor_tensor", ARG[:, 0:n], ARG[:, 0:n], M1[:, 0:n], ALU.add, r=[("ARG",), ("M1",)], w=[("ARG",)])
                O("act", "activation", hbuf[li][:, 0:n], ARG[:, 0:n], AF.Sin, r=[("ARG",)], w=[("HH", li % 2)])
                cur_in, kdim = hbuf[li], 64
                if li < 2:
                    cur_w = W2[:, li, :]
            hfin = hbuf[2]
            for tt in range(n // 128):
                tcg = g0 // 128 + tt
                pb = self.psb()
                O("pe", "matmul", self.PS[pb][:, :], hfin[:, tt * 128:(tt + 1) * 128], W3[:, :], start=True, stop=True,
                  r=[("HH", 0), ("W3",)], w=[("ps", pb)])
                DMA("sp", DEC[:, tcg % 2, :], self.k["decay_" + nm][:, tcg, :], w=[("DEC", tcg % 2)])
                O("dve", "tensor_tensor", HF[:, :].rearrange("p (a c) -> p a c", a=2), self.PS[pb][:, :].rearrange("p (a c) -> p a c", a=2),
                  DEC[:, tcg % 2, :].unsqueeze(1).to_broadcast([128, 2, 256]), ALU.mult, r=[("ps", pb), ("DEC", tcg % 2)], w=[("HF",)])
                if tcg == 0:
                    O("dve", "tensor_scalar", HF[:, 256:512], HF[:, 256:512], self.cols[:, 2:3], None, ALU.mult, r=[("HF",), ("cols",)], w=[("HF",)])
                O("dve", "tensor_tensor", RT[:, tcg, 0, :], HF[:, 0:256], HF[:, 256:512], ALU.add, r=[("HF",)], w=[("RT", tcg, 0)])
                O("dve", "tensor_tensor", RT[:, tcg, 2, :], HF[:, 256:512], HF[:, 0:256], ALU.subtract, r=[("HF",)], w=[("RT", tcg, 2)])

    def attention(self, b, l):
        p, O, DMA = self.p, self.O, self.DMA
        AQT = self.av(0, [128, 4, TOK], BF16)
        AKZ = self.av(18432, [128, 4, TOK], BF16)
        VX = self.av(36864, [128, NT, 2, 66], BF16)
        so = 41632
        SAs = [self.av(so + i * 2048, [128, 8, 128], BF16) for i in range(7)]
        RBF = self.rot_bufs(4, 2)
        O("pool", "memset", VX[:, :, :, 64:65], 1.0, w=[("VX1",)])
        for g in range(2):
            O("pool", "memset", AKZ[64:128, g * 2 + 0, :], 0.0, w=[("AKZ0", g, 0)])
            O("pool", "memset", AKZ[0:64, g * 2 + 1, :], 0.0, w=[("AKZ0", g, 1)])
        for i in range(6):
            if i < 4:
                self.load_win(l, SAs[i], 1536 + i * 128, 128, ("SA", i))
            else:
                g = i - 4
                for hlf in range(2):
                    DMA("sp", SAs[i][:, :, hlf * 64:(hlf + 1) * 64],
                        self.wb["w_in"][l, :, 2048 + g * 64: 2048 + (g + 1) * 64].rearrange("(kc p) n -> p kc n", p=128), w=[("SA", i, hlf)])
        self.load_win(l, SAs[6], 2176, 128, ("SA", 6))
        for gi, (t0, n) in enumerate(GROUPS):
            COS, SIN = RBF["TAB"][gi % 2]
            DMA("sp", COS[:, 0:n], self.k["atab"][0, :, t0:t0 + n], w=[("ATAB", gi % 2, 0)])
            DMA("sp", SIN[:, 0:n], self.k["atab"][1, :, t0:t0 + n], w=[("ATAB", gi % 2, 1)])
            tl = tiles_of(t0, n)
            for i in range(6):
                pb = self.psb()
                rk = [("SA", i)] if i < 4 else [("SA", i, 0), ("SA", i, 1)]
                for kc in range(8):
                    O("pe", "matmul", self.PS[pb][:, 0:n], SAs[i][:, kc, :], self.HT[:, kc, t0:t0 + n], start=(kc == 0), stop=(kc == 7),
                      r=rk + [("HT", kc, t) for t in tl], w=[("ps", pb)])
                if i < 4:
                    dst = AQT[:, i, t0:t0 + n]
                else:
                    g = i - 4
                    dst = [(0, 64, AKZ[0:64, g * 2 + 0, t0:t0 + n]), (64, 128, AKZ[64:128, g * 2 + 1, t0:t0 + n])]
                self.rotary(pb, n, dst, [("AQK", i, t) for t in tl], RBF, COS, SIN, [("ATAB", gi % 2, 0), ("ATAB", gi % 2, 1)])
        for t in range(NT):
            pb = self.proj_tm(SAs[6], 0, 128, t, ("SA", 6))
            O("act", "copy", VX[:, t, :, 0:64], self.PS[pb][:, 0:128].rearrange("p (g d) -> p g d", g=2), r=[("ps", pb), ("VX1",)], w=[("VX", t)])
        p.barrier()
        NPB = 6
        PB = [self.av(so + i * 1024, [128, 512], BF16) for i in range(NPB)]
        ATT = [self.av(so + 6144 + i * 2048, [128, 512], F32) for i in range(2)]
        ATB = [self.av(so + 10240 + i * 1024, [128, 512], BF16) for i in range(2)]
        JNK = self.av(so + 12288, [128, 512], BF16)
        DEN = self.av(so + 13312, [128, 8], F32)
        SS = self.av(so + 13344, [128, 4], F32)
        jobs = []
        for n in range(NT):
            if n < 16:
                kbs = ([(n - 1, self.mprev)] if n > 0 else []) + [(n, None)] + ([(n + 1, self.mnext)] if n < 15 else []) + [(16, None), (17, None)]
            else:
                kbs = [(16, None), (17, None)]
            for g in range(2):
                for ki, (kb, mask) in enumerate(kbs):
                    jobs.append((n, g, ki, len(kbs), kb, mask))
        LA = 2
        pend = {}
        accb = {}

        def unit_epilogue(n, g):
            pba = accb[(n, g)]
            acc = self.PS[pba][:, 0:264].rearrange("p (h d) -> p h d", h=4)
            att = ATT[n % 2]
            O("dve", "tensor_tensor", DEN[:, g * 4:(g + 1) * 4], acc[:, :, 64], self.SINKE[:, l * 8 + g * 4: l * 8 + g * 4 + 4], ALU.add,
              r=[("ps", pba), ("SINKE",)], w=[("DEN", g)])
            O("dve", "reciprocal", DEN[:, g * 4:(g + 1) * 4], DEN[:, g * 4:(g + 1) * 4], r=[("DEN", g)], w=[("DEN", g)])
            O("dve", "tensor_tensor", att[:, g * 256:(g + 1) * 256].rearrange("p (h d) -> p h d", h=4), acc[:, :, 0:64],
              DEN[:, g * 4:(g + 1) * 4].unsqueeze(2).to_broadcast([128, 4, 64]), ALU.mult, r=[("ps", pba), ("DEN", g)], w=[("ATT", n % 2, g)])

        def block_epilogue(n):
            att = ATT[n % 2]
            O("act", "activation", JNK[:, :], att[:, :], AF.Square, r=[("ATT", n % 2, 0), ("ATT", n % 2, 1)], w=[("JNK",)])
            O("dve", "reduce_sum", SS[:, 0:1], JNK[:, :], AX.X, r=[("JNK",)], w=[("SS", 0)])
            O("act", "activation", SS[:, 1:2], SS[:, 0:1], AF.Sqrt, bias=EPS, scale=1.0 / 512, r=[("SS", 0)], w=[("SS", 1)])
            O("dve", "reciprocal", SS[:, 1:2], SS[:, 1:2], r=[("SS", 1)], w=[("SS", 1)])
            atb = ATB[n % 2]
            O("dve", "tensor_scalar", atb[:, :], att[:, :], SS[:, 1:2], None, ALU.mult, r=[("ATT", n % 2, 0), ("ATT", n % 2, 1), ("SS", 1)], w=[("ATB", n % 2)])
            pbt = self.psb()
            tp = self.PS[pbt][:, 0:256].bitcast(BF16).rearrange("p (c t) -> p c t", c=4)
            for c in range(4):
                O("pe", "transpose", tp[:, c, :], atb[:, c * 128:(c + 1) * 128], self.ident_b[:, :], r=[("ATB", n % 2), ("ident_b",)], w=[("ps", pbt)])
            for c in range(4):
                O("dve", "tensor_scalar", self.MIXT[:, 4 + c, n * 128:(n + 1) * 128], tp[:, c, :], self.COLS[:, l, 36 + c:37 + c], None, ALU.mult,
                  r=[("ps", pbt), ("COLS", l)], w=[("MIXT", 4 + c, n)])

        for k in range(len(jobs) + LA):
            if k < len(jobs):
                (n, g, ki, nkb, kb, mask) = jobs[k]
                pbs = self.psb()
                for hh in range(4):
                    head = 4 * g + hh
                    c, hf = head // 2, head % 2
                    O("pe", "matmul", self.PS[pbs][:, hh * 128:(hh + 1) * 128], AKZ[:, g * 2 + hf, kb * 128:(kb + 1) * 128],
                      AQT[:, c, n * 128:(n + 1) * 128], start=True, stop=True,
                      r=[("AQK", 4 + g, kb), ("AQK", c, n), ("AKZ0", g, 0), ("AKZ0", g, 1)], w=[("ps", pbs)])
                P = PB[k % NPB]
                pk = ("P", k % NPB)
                O("act", "activation", P[:, :], self.PS[pbs][:, :], AF.Exp, bias=-ATT_M, scale=0.125, r=[("ps", pbs)], w=[pk])
                if mask is not None:
                    O("dve", "tensor_tensor", P[:, :].rearrange("p (h i) -> p h i", h=4), P[:, :].rearrange("p (h i) -> p h i", h=4),
                      mask[:, :].unsqueeze(1).to_broadcast([128, 4, 128]), ALU.mult, r=[pk, ("mprev",), ("mnext",)], w=[pk])
                pend[k] = (P, pk)
            j = k - LA
            if j >= 0:
                (n, g, ki, nkb, kb, mask) = jobs[j]
                P, pk = pend.pop(j)
                if ki == 0:
                    accb[(n, g)] = self.psb()
                pba = accb[(n, g)]
                acc = self.PS[pba][:, 0:264].rearrange("p (h d) -> p h d", h=4)
                for hh in range(4):
                    O("pe", "matmul", acc[:, hh, 0:65], P[:, hh * 128:(hh + 1) * 128], VX[:, kb, g, 0:65],
                      start=(ki == 0 and hh == 0), stop=(ki == nkb - 1 and hh == 3), skip_group_check=True,
                      r=[pk, ("VX", kb), ("VX1",)], w=[("ps", pba)])
                if ki == nkb - 1:
                    unit_epilogue(n, g)
                    if g == 1:
                        block_epilogue(n)

    def postnorm_resid(self, b, l, kG, TMPO, SQ, RS, Tt, t0, n, toff):
        O = self.O
        bcol = b if t0 < L else self.NB
        tl = tiles_of(t0, n)
        pb = self.psb()
        for oc in range(8):
            O("pe", "matmul", self.PS[pb][:, 0:n], self.ones_b[:, :], SQ[:, oc, toff:toff + n], start=(oc == 0), stop=(oc == 7),
              r=[("SQo", oc, toff), ("ones_b",)], w=[("ps", pb)])
        O("act", "activation", RS[:, 0:n], self.PS[pb][:, 0:n], AF.Sqrt, bias=EPS, scale=1.0 / D, r=[("ps", pb)], w=[("RSo",)])
        O("dve", "reciprocal", RS[:, 0:n], RS[:, 0:n], r=[("RSo",)], w=[("RSo",)])
        for oc in range(8):
            T = Tt[oc % 2]
            gcol = self.PAR[:, l, kG, oc, bcol:bcol + 1]
            O("dve", "scalar_tensor_tensor", T[:, 0:n], TMPO[:, oc, toff:toff + n], gcol, RS[:, 0:n], ALU.mult, ALU.mult,
              r=[("TMPO", oc, toff), ("RSo",), ("PAR", l, kG)], w=[("To", oc % 2)])
            xs = self.X[:, oc, t0:t0 + n]
            O("pool", "tensor_tensor", xs, xs, T[:, 0:n], ALU.add, r=[("To", oc % 2)] + [("X", oc, t) for t in tl], w=[("X", oc, t) for t in tl])

    def wout(self, b, l):
        p, O, DMA = self.p, self.O, self.DMA
        W = self.av(0, [128, 8, 1024], BF16)
        TMPO = self.av(16384, [128, 8, 512], F32)
        SQ = self.av(32768, [128, 8, 512], BF16)
        RS = self.av(40960, [128, 512], F32)
        Tt = [self.av(43008 + i * 2048, [128, 512], F32) for i in range(2)]
        for hlf in range(2):
            DMA("sp", W[:, :, hlf * 512:(hlf + 1) * 512], self.wb["w_out"][l, :, hlf * 512:(hlf + 1) * 512].rearrange("(kc p) n -> p kc n", p=128),
                w=[("W", hlf)])
        for (t0, n) in GROUPS:
            tl = tiles_of(t0, n)
            for oc in range(8):
                pb = self.psb()
                for kc in range(8):
                    O("pe", "matmul", self.PS[pb][:, 0:n], W[:, kc, oc * 128:(oc + 1) * 128], self.MIXT[:, kc, t0:t0 + n], start=(kc == 0), stop=(kc == 7),
                      r=[("W", oc // 4)] + [("MIXT", kc, t) for t in tl], w=[("ps", pb)])
                O("act", "copy", TMPO[:, oc, 0:n], self.PS[pb][:, 0:n], r=[("ps", pb)], w=[("TMPO", oc, 0)])
                O("dve", "tensor_tensor", SQ[:, oc, 0:n], self.PS[pb][:, 0:n], TMPO[:, oc, 0:n], ALU.mult, r=[("ps", pb), ("TMPO", oc, 0)], w=[("SQo", oc, 0)])
            self.postnorm_resid(b, l, PG1, TMPO, SQ, RS, Tt, t0, n, 0)

    def ffn(self, b, l):
        p, O, DMA = self.p, self.O, self.DMA
        AT = self.bv(0, [128, 32, 768])
        H2T = self.bv(24576, [128, 8, 768])
        SQF = self.bv(30720, [128, 8, 768])
        RING = [self.av(i * 8192, [128, 4096], BF16) for i in range(3)]
        TMPO = self.av(24576, [128, 8, 768], F32)
        RS = self.av(49152, [128, 512], F32)
        RL = [self.av(51200 + i * 2048, [128, 512], F32) for i in range(2)]
        ri = 0
        pref = {}
        for ps_i, groups in enumerate(FFN_PASSES):
            p.barrier()
            base_t0 = groups[0][0]
            self.prenorm_local(b, l, groups, H2T, base_t0, 24576)
            for s in range(8):
                if s in pref:
                    slab, key = pref.pop(s)
                else:
                    slab = RING[ri % 3].rearrange("p (k n) -> p k n", k=8)
                    key = ("RING", ri % 3)
                    ri += 1
                    DMA("sp", slab, self.wb["w_ff1"][l, :, s * 512:(s + 1) * 512].rearrange("(kc p) n -> p kc n", p=128), w=[key])
                for oc4 in range(4):
                    oc = s * 4 + oc4
                    for (t0, n) in groups:
                        lo = t0 - base_t0
                        pb = self.psb()
                        for kc in range(8):
                            O("pe", "matmul", self.PS[pb][:, 0:n], slab[:, kc, oc4 * 128:(oc4 + 1) * 128], H2T[:, kc, lo:lo + n], start=(kc == 0), stop=(kc == 7),
                              r=[key] + [("H2T", kc, t) for t in tiles_of(lo, n)], w=[("ps", pb)])
                        rl = RL[(oc + (lo > 0)) % 2]
                        rk = ("RL", (oc + (lo > 0)) % 2)
                        O("act", "activation", rl[:, 0:n], self.PS[pb][:, 0:n], AF.Relu, r=[("ps", pb)], w=[rk])
                        O("pool", "tensor_tensor", AT[:, oc, lo:lo + n], rl[:, 0:n], rl[:, 0:n], ALU.mult, r=[rk], w=[("AT", oc, lo)])
            for oc in range(8):
                slab = RING[ri % 3].rearrange("p (k n) -> p k n", k=32)
                key = ("RING", ri % 3)
                ri += 1
                DMA("sp", slab, self.wb["w_ff2"][l, :, oc * 128:(oc + 1) * 128].rearrange("(kc p) n -> p kc n", p=128), w=[key])
                for (t0, n) in groups:
                    lo = t0 - base_t0
                    pb = self.psb()
                    for kc in range(32):
                        O("pe", "matmul", self.PS[pb][:, 0:n], slab[:, kc, :], AT[:, kc, lo:lo + n], start=(kc == 0), stop=(kc == 31),
                          r=[key, ("AT", kc, lo)], w=[("ps", pb)])
                    O("act", "copy", TMPO[:, oc, lo:lo + n], self.PS[pb][:, 0:n], r=[("ps", pb)], w=[("TMPO", oc, lo)])
                    O("dve", "tensor_tensor", SQF[:, oc, lo:lo + n], self.PS[pb][:, 0:n], TMPO[:, oc, lo:lo + n], ALU.mult,
                      r=[("ps", pb), ("TMPO", oc, lo)], w=[("SQo", oc, lo)])
            if ps_i + 1 < len(FFN_PASSES):
                for s in range(2):
                    slab = RING[ri % 3].rearrange("p (k n) -> p k n", k=8)
                    key = ("RING", ri % 3)
                    ri += 1
                    DMA("sp", slab, self.wb["w_ff1"][l, :, s * 512:(s + 1) * 512].rearrange("(kc p) n -> p kc n", p=128), w=[key])
                    pref[s] = (slab, key)
            for (t0, n) in groups:
                self.postnorm_resid(b, l, PG2, TMPO, SQF, RS, RL, t0, n, t0 - base_t0)

    def prenorm_local(self, b, l, groups, H2T, base_t0, aoff):
        O = self.O
        SQ = self.av(aoff, [128, 8, 512], BF16)
        for gi, (t0, n) in enumerate(groups):
            RS = self.av(aoff + 8192 + (gi % 2) * 2048, [128, 512], F32)
            bcol = b if t0 < L else self.NB
            tl = tiles_of(t0, n)
            lo = t0 - base_t0
            for kc in range(8):
                xin = self.X[:, kc, t0:t0 + n]
                rk = [("X", kc, t) for t in tl]
                if kc % 2 == 0:
                    O("act", "activation", SQ[:, kc, 0:n], xin, AF.Square, r=rk, w=[("SQ", kc)])
                else:
                    O("pool", "tensor_tensor", SQ[:, kc, 0:n], xin, xin, ALU.mult, r=rk, w=[("SQ", kc)])
            pb = self.psb()
            for kc in range(8):
                O("pe", "matmul", self.PS[pb][:, 0:n], self.ones_b[:, :], SQ[:, kc, 0:n], start=(kc == 0), stop=(kc == 7),
                  r=[("SQ", kc), ("ones_b",)], w=[("ps", pb)])
            O("act", "activation", RS[:, 0:n], self.PS[pb][:, 0:n], AF.Sqrt, bias=EPS, scale=1.0 / D, r=[("ps", pb)], w=[("RS", gi % 2)])
            O("dve", "reciprocal", RS[:, 0:n], RS[:, 0:n], r=[("RS", gi % 2)], w=[("RS", gi % 2)])
            for kc in range(8):
                TMP = self.av(aoff + 12288 + (kc % 2) * 2048, [128, 512], F32)
                a_col = self.PAR[:, l, PA2, kc, bcol:bcol + 1]
                s_col = self.PAR[:, l, PS2, kc, bcol:bcol + 1]
                xin = self.X[:, kc, t0:t0 + n]
                O("dve", "scalar_tensor_tensor", TMP[:, 0:n], xin, a_col, RS[:, 0:n], ALU.mult, ALU.mult,
                  r=[("X", kc, t) for t in tl] + [("RS", gi % 2), ("PAR", l, PA2)], w=[("PNT", kc % 2)])
                O("act", "activation", H2T[:, kc, lo:lo + n], TMP[:, 0:n], AF.Identity, bias=s_col, scale=1.0,
                  r=[("PNT", kc % 2), ("PAR", l, PS2)], w=[("H2T", kc, t) for t in tiles_of(lo, n)])


_NC_CACHE = {}


def kernel(**inputs):
    NB = 4
    n_cores = 8
    key = (NB, DEPTH)
    if key not in _NC_CACHE:
        _NC_CACHE[key] = Builder(NB=NB, NL=DEPTH).build()
    nc = _NC_CACHE[key]
    consts = get_consts()
    in_maps = []
    for ci in range(n_cores):
        m = {}
        m["x"] = np.ascontiguousarray(inputs["x"][ci * NB:(ci + 1) * NB], dtype=np.float32)
        m["ctx"] = np.ascontiguousarray(inputs["ctx"][ci * NB:(ci + 1) * NB], dtype=np.float32)
        m["c"] = np.ascontiguousarray(inputs["c"][ci * NB:(ci + 1) * NB], dtype=np.float32)
        m["c_ctx"] = np.ascontiguousarray(np.asarray(inputs["c_ctx"], dtype=np.float32)[None, :])
        for name, _ in W_NAMES:
            m[name] = np.ascontiguousarray(inputs[name], dtype=np.float32)
        for name, arr in consts.items():
            m["k_" + name] = arr
        in_maps.append(m)
    res = run_bass_kernel_spmd(nc, in_maps, core_ids=list(range(n_cores)))
    out = np.concatenate([np.asarray(r["out"]) for r in res.results], axis=0)
    return out.astype(np.float32)
```
